# Optimizing a Trainium2 kernel written in Bass

```python
import math
import jax, jax.numpy as jnp
from jax import lax
import numpy as np

D_MODEL = 2048
BATCH = 16
SEQ = 2048
DEPTH = 2

N_MIXERS = 2
N_ATTN_LAYERS = (DEPTH + 1) // 2
N_LRU_LAYERS = DEPTH // 2
CHUNK = 64
LEFT_CHUNKS = 8
BAND = (LEFT_CHUNKS + 1) * CHUNK
ATT_HEADS = 16
ATT_HEAD_DIM = 128
D_ATT = ATT_HEADS * ATT_HEAD_DIM
MAX_REL_DIST = 256
N_REL = 2 * MAX_REL_DIST + 1
D_RNN = D_MODEL
LRU_BLOCKS = 16
LRU_BLOCK_W = D_RNN // LRU_BLOCKS
CONV_W = 4
RG_C = 8.0
LN_EPS = 1e-5
NEG_INF = -1e30
DEEPNORM_ALPHA = (2.0 * DEPTH) ** 0.25
DEEPNORM_BETA = (8.0 * DEPTH) ** -0.25

kernel_name = "chunked_attn_rglru_deepnorm_hybrid"


def _layer_norm(x, gain, bias):
    xf = x.astype(jnp.float32)
    mu = jnp.mean(xf, axis=-1, keepdims=True)
    var = jnp.mean(jnp.square(xf - mu), axis=-1, keepdims=True)
    y = (xf - mu) * lax.rsqrt(var + LN_EPS)
    return (y * gain.astype(jnp.float32) + bias.astype(jnp.float32)).astype(x.dtype)


def _rel_index():
    i = np.arange(CHUNK)[:, None]
    j = np.arange(BAND)[None, :]
    dist = i - (j - LEFT_CHUNKS * CHUNK)
    return (np.clip(dist, -MAX_REL_DIST, MAX_REL_DIST) + MAX_REL_DIST).astype(np.int32)


def _chunk_attention_mixer(h, w_in, w_out, rel_table):
    B, S, _ = h.shape
    nc = S // CHUNK
    proj = h @ w_in
    q, k, v, g = jnp.split(proj, 4, axis=-1)
    q = q.reshape(B, S, ATT_HEADS, ATT_HEAD_DIM) * (ATT_HEAD_DIM ** -0.5)
    pad = ((0, 0), (LEFT_CHUNKS * CHUNK, 0), (0, 0), (0, 0))
    kp = jnp.pad(k.reshape(B, S, ATT_HEADS, ATT_HEAD_DIM), pad)
    vp = jnp.pad(v.reshape(B, S, ATT_HEADS, ATT_HEAD_DIM), pad)
    qc = q.reshape(B, nc, CHUNK, ATT_HEADS, ATT_HEAD_DIM).transpose(1, 0, 2, 3, 4)
    bias = rel_table.astype(jnp.float32)[:, jnp.asarray(_rel_index())]
    band_pos = jnp.arange(BAND) - LEFT_CHUNKS * CHUNK

    def one_chunk(args):
        c, q_blk = args
        kb = lax.dynamic_slice_in_dim(kp, c * CHUNK, BAND, axis=1)
        vb = lax.dynamic_slice_in_dim(vp, c * CHUNK, BAND, axis=1)
        s = jnp.einsum('bqhd,bkhd->bhqk', q_blk, kb).astype(jnp.float32) + bias
        valid = (band_pos + c * CHUNK) >= 0
        s = jnp.where(valid[None, None, None, :], s, NEG_INF)
        p = jax.nn.softmax(s, axis=-1).astype(vb.dtype)
        return jnp.einsum('bhqk,bkhd->bqhd', p, vb)

    o = lax.map(one_chunk, (jnp.arange(nc), qc))
    o = o.transpose(1, 0, 2, 3, 4).reshape(B, S, D_ATT)
    return (o * jax.nn.silu(g)) @ w_out


def _lru_combine(left, right):
    a1, b1 = left
    a2, b2 = right
    return a1 * a2, a2 * b1 + b2


def _rglru_mixer(h, w_in, conv_w, conv_b, w_a, b_a, w_x, b_x, lam, w_out):
    B, S, _ = h.shape
    proj = h @ w_in
    u, g = jnp.split(proj, 2, axis=-1)
    up = jnp.pad(u, ((0, 0), (CONV_W - 1, 0), (0, 0)))
    u = conv_b + sum(up[:, t:t + S] * conv_w[t] for t in range(CONV_W))
    ub = u.reshape(B, S, LRU_BLOCKS, LRU_BLOCK_W)
    r = jax.nn.sigmoid(jnp.einsum('bsni,nij->bsnj', ub, w_a) + b_a).reshape(B, S, D_RNN)
    i = jax.nn.sigmoid(jnp.einsum('bsni,nij->bsnj', ub, w_x) + b_x).reshape(B, S, D_RNN)
    log_a = -RG_C * r.astype(jnp.float32) * jax.nn.softplus(-lam.astype(jnp.float32))
    a = jnp.exp(log_a)
    mult = jnp.sqrt(-jnp.expm1(2.0 * log_a))
    b = mult * (i * u).astype(jnp.float32)
    _, hs = lax.associative_scan(_lru_combine, (a, b), axis=1)
    y = hs.astype(h.dtype) * jax.nn.silu(g)
    return y @ w_out


def setup_inputs(seed: int = 0) -> dict:
    key = jax.random.key(seed)
    ks = jax.random.split(key, 16)
    f32 = jnp.float32
    x = jax.random.normal(ks[0], (BATCH, SEQ, D_MODEL), f32)

    attn_w_in = jax.random.normal(ks[1], (N_ATTN_LAYERS, D_MODEL, 4 * D_ATT), f32) * D_MODEL ** -0.5
    col_scale = jnp.concatenate([jnp.ones((2 * D_ATT,), f32),
                                 jnp.full((D_ATT,), DEEPNORM_BETA, f32),
                                 jnp.ones((D_ATT,), f32)])
    attn_w_in = attn_w_in * col_scale
    attn_w_out = jax.random.normal(ks[2], (N_ATTN_LAYERS, D_ATT, D_MODEL), f32) * (D_ATT ** -0.5) * DEEPNORM_BETA
    attn_rel_bias = jax.random.normal(ks[3], (N_ATTN_LAYERS, ATT_HEADS, N_REL), f32) * 0.1

    lru_w_in = jax.random.normal(ks[4], (N_LRU_LAYERS, D_MODEL, 2 * D_RNN), f32) * D_MODEL ** -0.5
    lru_conv_w = jax.random.normal(ks[5], (N_LRU_LAYERS, CONV_W, D_RNN), f32) * CONV_W ** -0.5
    lru_conv_b = jax.random.normal(ks[6], (N_LRU_LAYERS, D_RNN), f32) * 0.02
    lru_wa = jax.random.normal(ks[7], (N_LRU_LAYERS, LRU_BLOCKS, LRU_BLOCK_W, LRU_BLOCK_W), f32) * LRU_BLOCK_W ** -0.5
    lru_ba = jax.random.normal(ks[8], (N_LRU_LAYERS, LRU_BLOCKS, LRU_BLOCK_W), f32) * 0.02
    lru_wx = jax.random.normal(ks[9], (N_LRU_LAYERS, LRU_BLOCKS, LRU_BLOCK_W, LRU_BLOCK_W), f32) * LRU_BLOCK_W ** -0.5
    lru_bx = jax.random.normal(ks[10], (N_LRU_LAYERS, LRU_BLOCKS, LRU_BLOCK_W), f32) * 0.02
    a_c = jax.random.uniform(ks[11], (N_LRU_LAYERS, D_RNN), f32, 0.9, 0.999)
    s = a_c ** (1.0 / RG_C)
    lru_lambda = jnp.log(s) - jnp.log1p(-s)
    lru_w_out = jax.random.normal(ks[12], (N_LRU_LAYERS, D_RNN, D_MODEL), f32) * (D_RNN ** -0.5) * DEEPNORM_BETA

    ln_gain = 1.0 + 0.02 * jax.random.normal(ks[13], (DEPTH, D_MODEL), f32)
    ln_bias = 0.02 * jax.random.normal(ks[14], (DEPTH, D_MODEL), f32)
    return {"x": x, "attn_w_in": attn_w_in, "attn_w_out": attn_w_out, "attn_rel_bias": attn_rel_bias,
            "lru_w_in": lru_w_in, "lru_conv_w": lru_conv_w, "lru_conv_b": lru_conv_b,
            "lru_wa": lru_wa, "lru_ba": lru_ba, "lru_wx": lru_wx, "lru_bx": lru_bx,
            "lru_lambda": lru_lambda, "lru_w_out": lru_w_out,
            "ln_gain": ln_gain, "ln_bias": ln_bias}


def reference(x, attn_w_in, attn_w_out, attn_rel_bias, lru_w_in, lru_conv_w, lru_conv_b,
              lru_wa, lru_ba, lru_wx, lru_bx, lru_lambda, lru_w_out, ln_gain, ln_bias):
    h = x
    for layer in range(DEPTH):
        j = layer // N_MIXERS
        if layer % N_MIXERS == 0:
            y = _chunk_attention_mixer(h, attn_w_in[j], attn_w_out[j], attn_rel_bias[j])
        else:
            y = _rglru_mixer(h, lru_w_in[j], lru_conv_w[j], lru_conv_b[j], lru_wa[j], lru_ba[j],
                             lru_wx[j], lru_bx[j], lru_lambda[j], lru_w_out[j])
        h = _layer_norm(DEEPNORM_ALPHA * h + y, ln_gain[layer], ln_bias[layer])
    return h
```

```python
import numpy as np
from contextlib import ExitStack
import concourse.bass as bass
import concourse.mybir as mybir
from concourse.bass_utils import run_bass_kernel_spmd

F32 = mybir.dt.float32
BF16 = mybir.dt.bfloat16
AF = mybir.ActivationFunctionType
ALU = mybir.AluOpType

D = 2048
H = 16
ALPHA = (2.0 * 2) ** 0.25
LN_EPS = 1e-5
NEG = -30000.0
QSCALE = 128 ** -0.5
N_CORES = 8


class Ev:
    __slots__ = ("sem", "val", "eng")

    def __init__(self, sem, val, eng):
        self.sem, self.val, self.eng = sem, val, eng


class Sched:
    ENGS = ("pe", "act", "dve", "pool", "sp")
    DMAQ = ("sp", "pool", "act")

    def __init__(self, nc, stack, ndma=8):
        self.nc = nc
        self.sem = {e: stack.enter_context(nc.semaphore("c_" + e)) for e in self.ENGS}
        self.cnt = dict.fromkeys(self.ENGS, 0)
        self.prog = {e: [] for e in self.ENGS}
        self.known = {e: {} for e in self.ENGS}
        self.dsem = {q: [stack.enter_context(nc.semaphore("d_%s%d" % (q, i))) for i in range(ndma)]
                     for q in self.DMAQ}
        self.dcnt = {q: [0] * ndma for q in self.DMAQ}
        self.drr = dict.fromkeys(self.DMAQ, 0)
        self.tok = {}
        self.defer = {e: ([], []) for e in self.ENGS}
        self.last = dict.fromkeys(self.ENGS, None)
        self.bgsem = [stack.enter_context(nc.semaphore("bg%d" % i)) for i in range(8)]
        self.bgcnt = [0] * 8
        self.bgrr = 0

    def dma_bg(self, q, emit):
        i = self.bgrr
        self.bgrr = (i + 1) % 8
        sem = self.bgsem[i]
        prev = self.bgcnt[i]
        waits = self._need(q, [Ev(sem, prev, "dma")]) if prev else []
        self.bgcnt[i] = prev + 16
        self.prog[q].append((waits, emit, (sem, 16)))

    def _need(self, eng, evs):
        kn = self.known[eng]
        best = {}
        for ev in evs:
            if ev is None:
                continue
            if ev.eng == eng and eng == "pe":
                continue
            k = ev.sem
            if kn.get(k, 0) >= ev.val:
                continue
            if best.get(k, 0) < ev.val:
                best[k] = ev.val
        waits = []
        for k, v in best.items():
            kn[k] = v
            waits.append((k, v))
        return waits

    def _deps(self, reads, writes):
        evs = []
        for t in reads:
            st = self.tok.get(t)
            if st is not None and st[0] is not None:
                evs.append(st[0])
        for t in writes:
            st = self.tok.get(t)
            if st is not None:
                if st[0] is not None:
                    evs.append(st[0])
                evs.extend(st[1].values())
        return evs

    def _commit(self, ev, reads, writes):
        key = ev.sem
        for t in reads:
            st = self.tok.get(t)
            if st is None:
                st = self.tok[t] = [None, {}]
            st[1][key] = ev
        for t in writes:
            self.tok[t] = [ev, {}]

    def op(self, eng, emit, reads=(), writes=(), signal=True):
        evs = self._deps(reads, writes)
        waits = self._need(eng, evs)
        dr, dw = self.defer[eng]
        if signal:
            self.cnt[eng] += 1
            ev = Ev(self.sem[eng], self.cnt[eng], eng)
            self.prog[eng].append((waits, emit, (self.sem[eng], 1)))
            self._commit(ev, list(reads) + dr, list(writes) + dw)
            self.defer[eng] = ([], [])
            self.last[eng] = ev
        else:
            self.prog[eng].append((waits, emit, None))
            dr.extend(reads)
            dw.extend(writes)

    def dma(self, q, emit, reads=(), writes=()):
        evs = self._deps(reads, writes)
        i = self.drr[q]
        self.drr[q] = (i + 1) % len(self.dsem[q])
        sem = self.dsem[q][i]
        prev = self.dcnt[q][i]
        if prev:
            evs.append(Ev(sem, prev, "dma"))
        waits = self._need(q, evs)
        self.dcnt[q][i] = prev + 16
        ev = Ev(sem, prev + 16, "dma")
        self.prog[q].append((waits, emit, (sem, 16)))
        self._commit(ev, reads, writes)

    def barrier(self, bg=False):
        for e in self.ENGS:
            assert not self.defer[e][0] and not self.defer[e][1], e
        evs = [self.last[e] for e in self.ENGS if self.last[e] is not None]
        if bg:
            for i, c in enumerate(self.bgcnt):
                if c:
                    evs.append(Ev(self.bgsem[i], c, "dma"))
        for q in self.DMAQ:
            for i, c in enumerate(self.dcnt[q]):
                if c:
                    evs.append(Ev(self.dsem[q][i], c, "dma"))
        for e in self.ENGS:
            w = self._need(e, evs)
            if w:
                self.prog[e].append((w, None, None))
        self.tok = {}

    def emit_all(self):
        nc = self.nc
        with nc.Block() as block:
            def mk(name):
                def body(e):
                    for waits, emit, inc in self.prog[name]:
                        for s, v in waits:
                            e.wait_ge(s, v)
                        if emit is not None:
                            ins = emit(e)
                            if inc is not None:
                                ins.then_inc(inc[0], inc[1])
                return body
            block.tensor(mk("pe"))
            block.scalar(mk("act"))
            block.vector(mk("dve"))
            block.gpsimd(mk("pool"))
            block.sync(mk("sp"))


def build(NSEQ=2, S=2048, debug=False, stop_phase=99):
    NT = S // 512
    assert NT >= 2
    NG = S // 128
    nc = bass.Bass("TRN2", target_bir_lowering=False)
    dt_in = lambda name, shape, dt=F32: nc.dram_tensor(name, shape, dt, kind="ExternalInput").ap()
    x_d = dt_in("x", [NSEQ * S, D])
    awin = dt_in("awin", [D, 4 * D])
    awout = dt_in("awout", [D, D])
    tab_d = dt_in("tab", [H, 128, 640])
    lwin = dt_in("lwin", [D, 2 * D])
    lwout = dt_in("lwout", [D, D])
    wa_d = dt_in("wa", [16, 128, 128])
    wx_d = dt_in("wx", [16, 128, 128])
    vec_d = dt_in("vecs", [128, 8, 16])
    lng_d = dt_in("lng", [2, D])
    lnb_d = dt_in("lnb", [2, D])
    ident_d = dt_in("ident", [128, 128])
    out_d = nc.dram_tensor("out", [NSEQ * S, D], F32, kind="ExternalOutput").ap()
    skind = "ExternalOutput" if debug else "Internal"
    scr = lambda name, shape, dt: nc.dram_tensor(name, shape, dt, kind=skind).ap()
    QT_d = scr("QT_s", [H, 128, S], BF16)
    KT_d = scr("KT_s", [H, 128, S], BF16)
    V_d = scr("V_s", [S, D], BF16)
    G_d = scr("G_s", [S, D], BF16)
    OGT_d = scr("OGT_s", [D, S], BF16)
    H1_d = scr("H1_s", [S, D], F32)
    YT_d = scr("YT_s", [D, S], BF16)
    WB_d = [nc.dram_tensor("WB%d_s" % i, [D, D], BF16).ap() for i in range(2)]

    with ExitStack() as stack:
        sb = lambda name, shape, dt: stack.enter_context(nc.sbuf_tensor(name, shape, dt))
        actT = sb("actT", [128, 16, S], BF16)
        wbuf = [sb("wbuf%d" % i, [128, 16, 512], BF16) for i in range(2)]
        identf = sb("identf", [128, 128], F32)
        identb = sb("identb", [128, 128], BF16)
        vecs = sb("vecs_sb", [128, 8, 16], F32)
        cl = sb("cl_sb", [128, 4, 16], F32)
        hb = sb("hb_sb", [128, 2, 16], F32)
        wa_sb = sb("wa_sb", [128, 16, 128], BF16)
        wx_sb = sb("wx_sb", [128, 16, 128], BF16)
        PADW = 25088
        pad = sb("pad", [128, PADW], F32)
        pb = [stack.enter_context(nc.psum_tensor("pb%d" % i, [128, 512], F32)) for i in range(8)]
        sc = Sched(nc, stack)

        class PadAlloc:
            def __init__(self):
                self.off = 0

            def f32(self, n):
                a = pad[:, self.off:self.off + n]
                self.off += n
                assert self.off <= PADW, self.off
                return a

            def bf16(self, n):
                assert n % 2 == 0
                a = pad[:, self.off:self.off + n // 2].bitcast(BF16)
                self.off += n // 2
                assert self.off <= PADW, self.off
                return a

        wslot = [0]

        def load_piece(w_ap, col0, ncols=512, dst_col=0):
            s = wslot[0]
            wv = w_ap.rearrange("(kc p) n -> p kc n", p=128)
            toks = []
            for hf in range(2):
                src = wv[:, hf * 8:(hf + 1) * 8, col0:col0 + ncols]
                dst = wbuf[s][:, hf * 8:(hf + 1) * 8, dst_col:dst_col + ncols]
                tk = "wbuf%d_h%d_c%d" % (s, hf, dst_col)
                sc.dma("pool", lambda e, dst=dst, src=src: e.dma_start(out=dst, in_=src),
                       reads=(), writes=(tk,))
                toks.append(tk)
            return tuple(toks)

        def load_piece_bf16(wb_ap, col0):
            s = wslot[0]
            wv = wb_ap.rearrange("(kc p) n -> p kc n", p=128)
            toks = []
            for hf in range(2):
                src = wv[:, hf * 8:(hf + 1) * 8, col0:col0 + 512]
                dst = wbuf[s][:, hf * 8:(hf + 1) * 8, :]
                tk = "wbuf%d_h%d_c0" % (s, hf)
                sc.dma("act", lambda e, dst=dst, src=src: e.dma_start(out=dst, in_=src), writes=(tk,))
                toks.append(tk)
            return tuple(toks)

        def next_slot():
            wslot[0] ^= 1

        sc.dma("sp", lambda e: e.dma_start(out=identf[:], in_=ident_d), writes=("identf",))
        sc.dma("pool", lambda e: e.dma_start(out=identb[:], in_=ident_d), writes=("identb",))
        sc.dma("sp", lambda e: e.dma_start(out=vecs[:], in_=vec_d), writes=("vecs",))
        sc.dma("pool", lambda e: e.dma_start(out=wa_sb[:], in_=wa_d.rearrange("n i j -> i n j")),
               writes=("wa",))
        sc.dma("pool", lambda e: e.dma_start(out=wx_sb[:], in_=wx_d.rearrange("n i j -> i n j")),
               writes=("wx",))
        sc.op("act", lambda e: e.activation(out=cl[:, 2, :], in_=vecs[:, 7, :], func=AF.Exp, scale=-1.0),
              reads=("vecs",), writes=("cl2",))
        sc.op("act", lambda e: e.activation(out=cl[:, 3, :], in_=cl[:, 2, :], func=AF.Ln, bias=1.0),
              reads=("cl2",), writes=("cl3",))
        sc.op("dve", lambda e: e.tensor_scalar(cl[:, 0, :], cl[:, 3, :], -8.0, None, ALU.mult),
              reads=("cl3",), writes=("cl0",))
        sc.op("dve", lambda e: e.tensor_scalar(cl[:, 1, :], cl[:, 3, :], -4.0, None, ALU.mult),
              reads=("cl3",), writes=("cl1",))
        sc.op("dve", lambda e: e.tensor_scalar(hb[:, :, :], vecs[:, 5:7, :], 0.5, None, ALU.mult),
              reads=("vecs",), writes=("hb",))
        sc.barrier()

        def transposes_f32(src, src_tok, g, bank0, evac_flip, all_act=False, tok=False):
            for b in range(4):
                bank = bank0 + (b % 2)
                for kk in range(4):
                    o = pb[bank][:, kk * 128:(kk + 1) * 128]
                    i_ = src[:, (4 * b + kk) * 128:(4 * b + kk + 1) * 128]
                    sc.op("pe", lambda e, o=o, i_=i_: e.transpose(o, i_, identf[:]),
                          reads=(src_tok, "identf"), writes=("pb%d" % bank,), signal=(kk == 3))
                dst = actT[:, 4 * b:4 * b + 4, g * 128:(g + 1) * 128]
                srcp = pb[bank][:, :].rearrange("p (k t) -> p k t", k=4)
                if all_act or (b + evac_flip) % 2 == 0:
                    sc.op("act", lambda e, dst=dst, srcp=srcp: e.activation(out=dst, in_=srcp, func=AF.Copy),
                          reads=("pb%d" % bank,), writes=(("aT%d_%d" % (g, b),) if tok else ()))
                else:
                    sc.op("dve", lambda e, dst=dst, srcp=srcp: e.tensor_copy(dst, srcp),
                          reads=("pb%d" % bank,), writes=(("aT%d_%d" % (g, b),) if tok else ()))

        def transposes_b16(src, src_tok, g, bank0, evac_flip):
            for b in range(4):
                bank = bank0 + (b % 2)
                pv = pb[bank][:, 0:256].bitcast(BF16)
                for kk in range(4):
                    o = pv[:, kk * 128:(kk + 1) * 128]
                    i_ = src[:, (4 * b + kk) * 128:(4 * b + kk + 1) * 128]
                    sc.op("pe", lambda e, o=o, i_=i_: e.transpose(o, i_, identb[:]),
                          reads=(src_tok, "identb"), writes=("pb%d" % bank,), signal=(kk == 3))
                dst = actT[:, 4 * b:4 * b + 4, g * 128:(g + 1) * 128]
                srcp = pv.rearrange("p (k t) -> p k t", k=4)
                if (b + evac_flip) % 2 == 0:
                    sc.op("act", lambda e, dst=dst, srcp=srcp: e.activation(out=dst, in_=srcp, func=AF.Copy),
                          reads=("pb%d" % bank,), writes=())
                else:
                    sc.op("dve", lambda e, dst=dst, srcp=srcp: e.tensor_copy(dst, srcp),
                          reads=("pb%d" % bank,), writes=())

        for sq in range(NSEQ):
            row0 = sq * S
            pa = PadAlloc()
            xs = [pa.f32(2048) for _ in range(4)]
            stg = [pa.bf16(4 * 512) for _ in range(2)]
            for g in range(NG):
                xb = xs[g % 4]
                src = x_d[row0 + g * 128: row0 + (g + 1) * 128, :]
                sc.dma("sp", lambda e, xb=xb, src=src: e.dma_start(out=xb, in_=src),
                       writes=("xs%d" % (g % 4),))
                transposes_f32(xb, "xs%d" % (g % 4), g, 2 * (g % 4), g, tok=True)
            if stop_phase <= 0:
                break
            nstage = [0]
            for p in range(16):
                s = wslot[0]
                wtoks = load_piece(awin, p * 512)
                if sq == 0 and p < 8:
                    wsrc = (awout if p < 4 else lwout)[:, (p % 4) * 512:(p % 4 + 1) * 512]
                    wdst = WB_d[p // 4][:, (p % 4) * 512:(p % 4 + 1) * 512]
                    sc.dma_bg("pool", lambda e, wdst=wdst, wsrc=wsrc: e.dma_start(out=wdst, in_=wsrc))
                kind_p = p // 4
                hp = p % 4
                for t in range(NT):
                    si = nstage[0] % 2
                    nstage[0] += 1
                    stv = stg[si].rearrange("p (j n) -> p j n", j=4)
                    stok = "stg%d" % si
                    for j in range(4):
                        bank = (4 * (t % 2) + j)
                        btok = "pb%d" % bank
                        for kc in range(16):
                            if kind_p < 2:
                                lhsT = wbuf[s][:, kc, j * 128:(j + 1) * 128]
                                rhs = actT[:, kc, t * 512:(t + 1) * 512]
                                atoks = tuple("aT%d_%d" % (4 * t + gg_, kc // 4) for gg_ in range(4))
                            else:
                                lhsT = actT[:, kc, t * 512 + j * 128: t * 512 + (j + 1) * 128]
                                rhs = wbuf[s][:, kc, :]
                                atoks = ("aT%d_%d" % (4 * t + j, kc // 4),)
                            sc.op("pe", lambda e, o=pb[bank][:, :], l=lhsT, r=rhs, kc=kc:
                                  e.matmul(o, l, r, start=(kc == 0), stop=(kc == 15)),
                                  reads=wtoks + (atoks if p == 0 else ()), writes=(btok,), signal=(kc == 15))
                        o = stv[:, j, :]
                        if kind_p == 0:
                            sc.op("act", lambda e, o=o, i_=pb[bank][:, :]: e.activation(
                                out=o, in_=i_, func=AF.Copy, scale=QSCALE),
                                reads=(btok,), writes=(stok,))
                        elif kind_p == 3:
                            sc.op("act", lambda e, o=o, i_=pb[bank][:, :]: e.activation(
                                out=o, in_=i_, func=AF.Silu), reads=(btok,), writes=(stok,))
                        else:
                            sc.op("dve", lambda e, o=o, i_=pb[bank][:, :]: e.tensor_copy(o, i_),
                                  reads=(btok,), writes=(stok,))
                    if kind_p < 2:
                        dd = (QT_d if kind_p == 0 else KT_d)[4 * hp:4 * hp + 4, :, t * 512:(t + 1) * 512]
                        dd = dd.rearrange("j p n -> p j n")
                    else:
                        dd = (V_d if kind_p == 2 else G_d)[t * 512:(t + 1) * 512, hp * 512:(hp + 1) * 512]
                        dd = dd.rearrange("(j p) n -> p j n", p=128)
                    sc.dma("sp", lambda e, dd=dd, stv=stv: e.dma_start(out=dd, in_=stv),
                           reads=(stok,), writes=())
                next_slot()
            sc.barrier()
            if stop_phase <= 1:
                continue

            pa = PadAlloc()
            ktb = [pa.bf16(S) for _ in range(2)]
            qtb = [pa.bf16(S) for _ in range(2)]
            vb = [pa.bf16(NG * 132).rearrange("p (t d) -> p t d", d=132) for _ in range(2)]
            gb = [pa.bf16(NG * 128).rearrange("p (t d) -> p t d", d=128) for _ in range(2)]
            tabb = [pa.bf16(640) for _ in range(2)]
            NSS = 5
            ssb = [pa.f32(512) for _ in range(NSS)]
            pTb = [pa.bf16(2560) for _ in range(2)]
            ogtok = [pa.bf16(128) for _ in range(8)]
            rcb = [pa.f32(2) for _ in range(8)]
            ogTs = [pa.bf16(S) for _ in range(2)]
            STB = (0, 1, 2, 5, 6)
            import os
            PREF = int(os.environ.get("PREF", "1"))
            NSTB = int(os.environ.get("NSTB", "5"))
            for i in range(2):
                sc.op("pool", lambda e, a=vb[i][:, :, 128:129]: e.memset(a, 1.0), writes=("v%d" % i,))

            def head_loads(h):
                hb = h % 2
                sc.dma("sp", lambda e, o=ktb[hb], i_=KT_d[h]: e.dma_start(out=o, in_=i_), writes=("kt%d" % hb,))
                sc.dma("sp", lambda e, o=qtb[hb], i_=QT_d[h]: e.dma_start(out=o, in_=i_), writes=("qt%d" % hb,))
                sc.dma("sp", lambda e, o=vb[hb][:, :, 0:128],
                       i_=V_d[:, h * 128:(h + 1) * 128].rearrange("(t p) d -> p t d", p=128):
                       e.dma_start(out=o, in_=i_), writes=("v%d" % hb,))
                sc.dma("sp", lambda e, o=gb[hb],
                       i_=G_d[:, h * 128:(h + 1) * 128].rearrange("(t p) d -> p t d", p=128):
                       e.dma_start(out=o, in_=i_), writes=("g%d" % hb,))
                sc.dma("pool", lambda e, o=tabb[hb], i_=tab_d[h]: e.dma_start(out=o, in_=i_), writes=("tab%d" % hb,))
                sc.op("pool", lambda e, a=tabb[hb][0:64, 576:640]: e.memset(a, NEG), writes=("tab%d" % hb,))
                sc.op("pool", lambda e, a=tabb[hb][64:128, 0:64]: e.memset(a, NEG), writes=("tab%d" % hb,))

            items2 = [(h, qi) for h in range(H) for qi in range(NT)]
            infos = {}
            nss = [0]

            def p2_f1(n):
                h, qi = items2[n]
                hb = h % 2
                if PREF and qi == 1 and h + 1 < H:
                    head_loads(h + 1)
                if (not PREF) and qi == 0 and h > 0:
                    head_loads(h)
                q0 = qi * 512
                pi = n % 2
                ptok = "pT%d" % pi
                info = {}
                off = 0
                for j in range(8):
                    J = 4 * qi - 4 + j
                    if J < 0:
                        continue
                    qlo = max(0, 2 * j - 8)
                    qhi = min(7, 2 * j + 1)
                    w = 64 * (qhi - qlo + 1)
                    c0 = 64 * (qlo - 2 * j + 8)
                    info[j] = (J, qlo, w, off)
                    k_ = nss[0]
                    nss[0] += 1
                    bank = STB[k_ % NSTB]
                    si = k_ % NSS
                    sc.op("pe", lambda e, o=pb[bank][:, 0:w], l=ktb[hb][:, J * 128:(J + 1) * 128],
                          r=qtb[hb][:, q0 + 64 * qlo: q0 + 64 * qlo + w]:
                          e.matmul(o, l, r, start=True, stop=False),
                          reads=("kt%d" % hb, "qt%d" % hb), writes=("pb%d" % bank,), signal=False)
                    sc.op("pe", lambda e, o=pb[bank][:, 0:w], l=identb[:], r=tabb[hb][:, c0:c0 + w]:
                          e.matmul(o, l, r, start=False, stop=True),
                          reads=("identb", "tab%d" % hb), writes=("pb%d" % bank,))
                    sc.op("act", lambda e, o=pTb[pi][:, off:off + w], a=pb[bank][:, 0:w]:
                          e.activation(out=o, in_=a, func=AF.Exp),
                          reads=("pb%d" % bank,), writes=(ptok,))
                    off += w
                infos[n] = info

            def p2_f2(n):
                h, qi = items2[n]
                hb = h % 2
                pi = n % 2
                ptok = "pT%d" % pi
                info = infos.pop(n)
                for g in range(4):
                    js = [j for j in range(g, g + 5) if j in info]
                    obank = 3 + (g % 2)
                    ocol = 0
                    otok = "pb%d" % obank
                    oap = pb[obank][:, ocol:ocol + 129]
                    for idx, j in enumerate(js):
                        J, qlo, w, poff = info[j]
                        c = poff + 64 * (2 * g - qlo)
                        sc.op("pe", lambda e, o=oap, l=pTb[pi][:, c:c + 128], r=vb[hb][:, J, 0:129],
                              st=(idx == 0), sp_=(idx == len(js) - 1):
                              e.matmul(o, l, r, start=st, stop=sp_),
                              reads=(ptok, "v%d" % hb), writes=(otok,), signal=(idx == len(js) - 1))
                    oi = (n % 2) * 4 + g
                    sc.op("dve", lambda e, o=rcb[oi][:, 0:1], a=pb[obank][:, ocol + 128:ocol + 129]:
                          e.reciprocal(o, a), reads=(otok,), writes=("rc%d" % oi,))
                    sc.op("dve", lambda e, o=ogtok[oi], a=pb[obank][:, ocol:ocol + 128], s_=rcb[oi][:, 0:1],
                          b=gb[hb][:, 4 * qi + g, :]:
                          e.scalar_tensor_tensor(o, a, s_, b, ALU.mult, ALU.mult),
                          reads=(otok, "rc%d" % oi, "g%d" % hb), writes=("ogtok%d" % oi,))

            def p2_f3(n):
                h, qi = items2[n]
                hb = h % 2
                q0 = qi * 512
                for g in range(4):
                    oi = (n % 2) * 4 + g
                    tpo = pb[7][:, 0:256].bitcast(BF16)[:, g * 128:(g + 1) * 128]
                    sc.op("pe", lambda e, o=tpo, a=ogtok[oi]: e.transpose(o, a, identb[:]),
                          reads=("ogtok%d" % oi, "identb"), writes=("pb7",), signal=(g == 3))
                sc.op("act", lambda e, o=ogTs[hb][:, q0:q0 + 512], a=pb[7][:, 0:256].bitcast(BF16):
                      e.activation(out=o, in_=a, func=AF.Copy),
                      reads=("pb7",), writes=("ogTs%d" % hb,))
                if qi == NT - 1:
                    sc.dma("sp", lambda e, o=OGT_d[h * 128:(h + 1) * 128, :], a=ogTs[hb]:
                           e.dma_start(out=o, in_=a), reads=("ogTs%d" % hb,))

            head_loads(0)
            NI = len(items2)
            import os
            SK1 = int(os.environ.get("SK1", "1"))
            SK2 = int(os.environ.get("SK2", "2"))
            for step in range(NI + SK2):
                if step < NI:
                    p2_f1(step)
                if 0 <= step - SK1 < NI:
                    p2_f2(step - SK1)
                if 0 <= step - SK2 < NI:
                    p2_f3(step - SK2)
            sc.barrier(bg=True)
            if stop_phase <= 2:
                continue

            def outproj_ln(srcT_d, w_d, layer, resid_d, resid_row0, dst_d, dst_row0, do_T):
                pa = PadAlloc()
                ogt = pa.bf16(16 * 512).rearrange("p (k n) -> p k n", k=16)
                xz = [[pa.f32(2048) for _ in range(4)] for _ in range(2)]
                gbc = pa.f32(2048)
                bbc = pa.f32(2048)
                stats = [pa.f32(24) for _ in range(2)]
                mv = [pa.f32(8) for _ in range(2)]
                nbk = [0]
                sc.dma("sp", lambda e: e.dma_start(out=gbc, in_=lng_d[layer, :].partition_broadcast(128)),
                       writes=("gbc",))
                sc.dma("sp", lambda e: e.dma_start(out=bbc, in_=lnb_d[layer, :].partition_broadcast(128)),
                       writes=("bbc",))
                sv = srcT_d.rearrange("(kc p) n -> p kc n", p=128)
                xtok = lambda t, i: "xz%d_%d" % (t % 2, i)

                def ogt_load(t):
                    for i in range(4):
                        sc.dma("sp", lambda e, o=ogt[:, :, i * 128:(i + 1) * 128],
                               i_=sv[:, :, t * 512 + i * 128: t * 512 + (i + 1) * 128]: e.dma_start(out=o, in_=i_),
                               writes=("ogtq%d" % i,))

                def xz_loads(t):
                    for i in range(4):
                        r0 = resid_row0 + t * 512 + i * 128
                        sc.dma("sp", lambda e, o=xz[t % 2][i], i_=resid_d[r0:r0 + 128, :]:
                               e.dma_start(out=o, in_=i_), writes=(xtok(t, i),))

                wq = []

                def prefetch(n):
                    s_ = wslot[0]
                    wq.append((s_, load_piece_bf16(w_d, n * 512)))
                    next_slot()

                def mm_piece(t, n):
                    s, wtoks = wq.pop(0)
                    for i in range(4):
                        bank = nbk[0] % 6
                        nbk[0] += 1
                        btok = "pb%d" % bank
                        for kc in range(16):
                            sc.op("pe", lambda e, o=pb[bank][:, :], l=ogt[:, kc, i * 128:(i + 1) * 128],
                                  r=wbuf[s][:, kc, :], kc=kc:
                                  e.matmul(o, l, r, start=(kc == 0), stop=(kc == 15)),
                                  reads=wtoks + ("ogtq%d" % i,), writes=(btok,), signal=(kc == 15))
                        zz = xz[t % 2][i][:, n * 512:(n + 1) * 512]
                        sc.op("dve", lambda e, zz=zz, a=pb[bank][:, :]:
                              e.scalar_tensor_tensor(zz, zz, ALPHA, a, ALU.mult, ALU.add),
                              reads=(btok, xtok(t, i)), writes=(xtok(t, i),))
                    if 4 * t + n + 2 < 4 * NT:
                        prefetch((n + 2) % 4)

                def ln_group(t, i):
                    si = i % 2
                    xt = xtok(t, i)
                    z = xz[t % 2][i]
                    for c4 in range(4):
                        sc.op("dve", lambda e, o=stats[si][:, c4 * 6:(c4 + 1) * 6],
                              a=z[:, c4 * 512:(c4 + 1) * 512]: e.bn_stats(o, a),
                              reads=(xt,), writes=("stats%d" % si,))
                    sc.op("dve", lambda e, o=mv[si][:, 0:2], a=stats[si]: e.bn_aggr(o, a),
                          reads=("stats%d" % si,), writes=("mv%d" % si,))
                    sc.op("dve", lambda e, o=mv[si][:, 2:3], a=mv[si][:, 1:2]:
                          e.tensor_scalar(o, a, LN_EPS, None, ALU.add),
                          reads=("mv%d" % si,), writes=("mvb%d" % si,))
                    sc.op("act", lambda e, o=mv[si][:, 3:4], a=mv[si][:, 2:3]:
                          e.activation(out=o, in_=a, func=AF.Sqrt),
                          reads=("mvb%d" % si,), writes=("mvc%d" % si,))
                    sc.op("dve", lambda e, o=mv[si][:, 4:5], a=mv[si][:, 3:4]: e.reciprocal(o, a),
                          reads=("mvc%d" % si,), writes=("mvd%d" % si,))
                    sc.op("dve", lambda e, z=z, m=mv[si][:, 0:1]:
                          e.scalar_tensor_tensor(z, z, m, gbc, ALU.subtract, ALU.mult),
                          reads=(xt, "mv%d" % si, "gbc"), writes=(xt,))
                    sc.op("act", lambda e, z=z, r=mv[si][:, 4:5]:
                          e.activation(out=z, in_=z, func=AF.Identity, scale=r),
                          reads=(xt, "mvd%d" % si), writes=(xt,))
                    sc.op("pool", lambda e, z=z: e.tensor_tensor(z, z, bbc, ALU.add),
                          reads=(xt, "bbc"), writes=(xt,))
                    r0 = dst_row0 + t * 512 + i * 128
                    sc.dma("sp", lambda e, o=dst_d[r0:r0 + 128, :], a=z: e.dma_start(out=o, in_=a),
                           reads=(xt,))

                def tr_group(t, i):
                    transposes_f32(xz[t % 2][i], xtok(t, i), 4 * t + i, 6, 0, all_act=True)

                ogt_load(0)
                xz_loads(0)
                prefetch(0)
                prefetch(1)
                for t in range(NT):
                    mm_piece(t, 0)
                    if t > 0:
                        ln_group(t - 1, 0)
                        ln_group(t - 1, 1)
                    mm_piece(t, 1)
                    if t > 0:
                        ln_group(t - 1, 2)
                        ln_group(t - 1, 3)
                    mm_piece(t, 2)
                    if t > 0 and do_T:
                        for i in range(4):
                            tr_group(t - 1, i)
                    if t + 1 < NT:
                        xz_loads(t + 1)
                    mm_piece(t, 3)
                    if t + 1 < NT:
                        ogt_load(t + 1)
                for i in range(4):
                    ln_group(NT - 1, i)
                    if do_T:
                        tr_group(NT - 1, i)
                sc.barrier()

            outproj_ln(OGT_d, WB_d[0], 0, x_d, row0, H1_d, 0, True)
            if stop_phase <= 3:
                continue

            pa = PadAlloc()
            upad = [pa.f32(S + 4) for _ in range(2)]
            ysb = [pa.bf16(S) for _ in range(2)]
            NTB = 3
            uc = [pa.f32(512) for _ in range(NTB)]
            ucb = [pa.bf16(512) for _ in range(NTB)]
            rr = [pa.f32(512) for _ in range(NTB)]
            ii = [pa.f32(512) for _ in range(NTB)]
            aa = [pa.f32(512) for _ in range(NTB)]
            mm = [pa.f32(512) for _ in range(NTB)]
            bb = [pa.f32(512) for _ in range(NTB)]
            hh = [pa.f32(512) for _ in range(NTB)]
            sg = [pa.f32(512) for _ in range(NTB)]
            gg = [pa.f32(512) for _ in range(NTB)]
            for i in range(2):
                sc.op("pool", lambda e, a=upad[i][:, 0:3]: e.memset(a, 0.0), writes=("upad%d" % i,))
            wt4 = {}

            def p4_wload(c2):
                wt4[c2] = (wslot[0],
                           load_piece(lwin, c2 * 256, 256, 0) + load_piece(lwin, D + c2 * 256, 256, 256))
                next_slot()

            NI4 = 16 * NT

            def p4_ids(n):
                c, t = n // NT, n % NT
                return c, t, c % 2, n % NTB, "_%d" % (n % NTB)

            def p4_f1a(n):
                c, t, ub, tb, T = p4_ids(n)
                c2, jj = c // 2, c % 2
                if jj == 0 and t == 0 and c2 + 1 < 8:
                    p4_wload(c2 + 1)
                s, wtoks = wt4[c2]
                bu, bg = 2 * (n % 2), 2 * (n % 2) + 1
                for (bank, colb) in ((bu, jj * 128), (bg, 256 + jj * 128)):
                    for kc in range(16):
                        sc.op("pe", lambda e, o=pb[bank][:, :], l=wbuf[s][:, kc, colb:colb + 128],
                              r=actT[:, kc, t * 512:(t + 1) * 512], kc=kc:
                              e.matmul(o, l, r, start=(kc == 0), stop=(kc == 15)),
                              reads=wtoks, writes=("pb%d" % bank,), signal=(kc == 15))

            def p4_f1b(n):
                c, t, ub, tb, T = p4_ids(n)
                bu, bg = 2 * (n % 2), 2 * (n % 2) + 1
                utok = "upad%d" % ub
                sc.op("act", lambda e, o=upad[ub][:, 3 + t * 512: 3 + (t + 1) * 512], a=pb[bu][:, :]:
                      e.activation(out=o, in_=a, func=AF.Copy),
                      reads=("pb%d" % bu,), writes=(utok + "_t%d" % t,))
                sc.op("act", lambda e, o=sg[tb], a=pb[bg][:, :]: e.activation(out=o, in_=a, func=AF.Tanh, scale=0.5),
                      reads=("pb%d" % bg,), writes=("sg" + T,))
                sc.op("dve", lambda e, o=gg[tb], a=pb[bg][:, :]: e.tensor_copy(o, a),
                      reads=("pb%d" % bg, "sg" + T), writes=("gg" + T,))

            def p4_f2(n):
                c, t, ub, tb, T = p4_ids(n)
                utok = "upad%d" % ub
                br, bi = 4 + 2 * (n % 2), 5 + 2 * (n % 2)
                ureads = (utok, utok + "_t%d" % t) + ((utok + "_t%d" % (t - 1),) if t > 0 else ())
                sc.op("dve", lambda e, o=uc[tb], a=upad[ub][:, t * 512: t * 512 + 512],
                      w0=vecs[:, 0, c:c + 1], cb=vecs[:, 4, c:c + 1]:
                      e.tensor_scalar(o, a, w0, cb, ALU.mult, ALU.add),
                      reads=ureads + ("vecs",), writes=("uc" + T,))
                for k in range(1, 4):
                    sc.op("dve", lambda e, o=uc[tb], a=upad[ub][:, t * 512 + k: t * 512 + k + 512],
                          wk=vecs[:, k, c:c + 1]:
                          e.scalar_tensor_tensor(o, a, wk, o, ALU.mult, ALU.add),
                          reads=ureads + ("vecs", "uc" + T), writes=("uc" + T,))
                sc.op("act", lambda e, o=ucb[tb], a=uc[tb]: e.activation(out=o, in_=a, func=AF.Copy),
                      reads=("uc" + T,), writes=("ucb" + T,))
                sc.op("pe", lambda e, o=pb[br][:, :], l=wa_sb[:, c, :], r=ucb[tb]:
                      e.matmul(o, l, r, start=True, stop=True),
                      reads=("wa", "ucb" + T), writes=("pb%d" % br,))
                sc.op("pe", lambda e, o=pb[bi][:, :], l=wx_sb[:, c, :], r=ucb[tb]:
                      e.matmul(o, l, r, start=True, stop=True),
                      reads=("wx", "ucb" + T), writes=("pb%d" % bi,))

            def p4_f3a(n):
                c, t, ub, tb, T = p4_ids(n)
                br, bi = 4 + 2 * (n % 2), 5 + 2 * (n % 2)
                sc.op("act", lambda e, o=rr[tb], a=pb[br][:, :], b=hb[:, 0, c:c + 1]:
                      e.activation(out=o, in_=a, func=AF.Tanh, bias=b, scale=0.5),
                      reads=("pb%d" % br, "hb"), writes=("rr" + T,))
                sc.op("act", lambda e, o=ii[tb], a=pb[bi][:, :], b=hb[:, 1, c:c + 1]:
                      e.activation(out=o, in_=a, func=AF.Tanh, bias=b, scale=0.5),
                      reads=("pb%d" % bi, "hb"), writes=("ii" + T,))
                sc.op("act", lambda e, o=aa[tb], a=rr[tb], s_=cl[:, 1, c:c + 1]:
                      e.activation(out=o, in_=a, func=AF.Exp, scale=s_, bias=s_),
                      reads=("rr" + T, "cl1"), writes=("aa" + T,))
                sc.op("act", lambda e, o=mm[tb], a=rr[tb], s_=cl[:, 0, c:c + 1]:
                      e.activation(out=o, in_=a, func=AF.Exp, scale=s_, bias=s_),
                      reads=("rr" + T, "cl0"), writes=("mm" + T,))
                sc.op("dve", lambda e, o=bb[tb], a=ii[tb], b=uc[tb]:
                      e.scalar_tensor_tensor(o, a, 1.0, b, ALU.add, ALU.mult),
                      reads=("ii" + T, "uc" + T), writes=("bb" + T,))
                sc.op("pool", lambda e, o=mm[tb]: e.tensor_scalar(o, o, 1.0, 0.0, ALU.min, ALU.max),
                      reads=("mm" + T,), writes=("mm" + T,))
                sc.op("act", lambda e, o=mm[tb]: e.activation(out=o, in_=o, func=AF.Sqrt, scale=-0.25, bias=0.25),
                      reads=("mm" + T,), writes=("mm" + T,))
                sc.op("pool", lambda e, o=bb[tb], a=mm[tb]: e.tensor_tensor(o, o, a, ALU.mult),
                      reads=("bb" + T, "mm" + T), writes=("bb" + T,))
                sc.op("pool", lambda e, o=sg[tb]: e.tensor_scalar(o, o, 0.5, 0.5, ALU.mult, ALU.add),
                      reads=("sg" + T,), writes=("sg" + T,))
                sc.op("pool", lambda e, o=sg[tb], a=gg[tb]: e.tensor_tensor(o, o, a, ALU.mult),
                      reads=("sg" + T, "gg" + T), writes=("sg" + T,))

            def p4_f3b(n):
                c, t, ub, tb, T = p4_ids(n)
                ptb = (n - 1) % NTB
                init = 0.0 if t == 0 else hh[ptb][:, 511:512]
                sc.op("dve", lambda e, o=hh[tb], a=aa[tb], b=bb[tb], init=init:
                      e.tensor_tensor_scan(o, a, b, init, ALU.mult, ALU.add),
                      reads=("aa" + T, "bb" + T) + (("hh_%d" % ptb,) if t > 0 else ()),
                      writes=("hh" + T,))
                sc.op("pool", lambda e, o=ysb[ub][:, t * 512:(t + 1) * 512], a=hh[tb], b=sg[tb]:
                      e.tensor_tensor(o, a, b, ALU.mult),
                      reads=("hh" + T, "sg" + T), writes=("ysb%d" % ub,))
                if t == NT - 1:
                    sc.dma("sp", lambda e, o=YT_d[c * 128:(c + 1) * 128, :], a=ysb[ub]: e.dma_start(out=o, in_=a),
                           reads=("ysb%d" % ub,))

            p4_wload(0)
            for step in range(NI4 + 2):
                if step < NI4:
                    p4_f1a(step)
                if 0 <= step - 2 < NI4:
                    p4_f3a(step - 2)
                if 0 <= step - 1 < NI4:
                    p4_f2(step - 1)
                if 0 <= step - 2 < NI4:
                    p4_f3b(step - 2)
                if step < NI4:
                    p4_f1b(step)
            sc.barrier()
            if stop_phase <= 4:
                continue
            outproj_ln(YT_d, WB_d[1], 1, H1_d, 0, out_d, row0, False)

        sc.barrier()
        sc.emit_all()
    return nc


def _prep_shared(inp):
    f = lambda a: np.ascontiguousarray(np.asarray(a, dtype=np.float32))
    k = np.arange(128)[:, None]
    xq = np.arange(640)[None, :]
    idx = np.minimum(xq - k, 256) + 256
    tab = f(np.asarray(inp["attn_rel_bias"])[0][:, idx])
    cw = np.asarray(inp["lru_conv_w"])[0]
    vec = np.stack([cw[0], cw[1], cw[2], cw[3], np.asarray(inp["lru_conv_b"])[0],
                    np.asarray(inp["lru_ba"])[0].reshape(-1), np.asarray(inp["lru_bx"])[0].reshape(-1),
                    np.asarray(inp["lru_lambda"])[0]], axis=0)
    vecs = f(vec.reshape(8, 16, 128).transpose(2, 0, 1))
    return {
        "awin": f(np.asarray(inp["attn_w_in"])[0]), "awout": f(np.asarray(inp["attn_w_out"])[0]),
        "tab": tab, "lwin": f(np.asarray(inp["lru_w_in"])[0]), "lwout": f(np.asarray(inp["lru_w_out"])[0]),
        "wa": f(np.asarray(inp["lru_wa"])[0]), "wx": f(np.asarray(inp["lru_wx"])[0]), "vecs": vecs,
        "lng": f(inp["ln_gain"]), "lnb": f(inp["ln_bias"]), "ident": np.eye(128, dtype=np.float32),
    }


def kernel(**inputs):
    x = np.asarray(inputs["x"], dtype=np.float32)
    B, S, _ = x.shape
    nseq = B // N_CORES
    shared = _prep_shared(inputs)
    nc = build(NSEQ=nseq, S=S)
    in_maps = []
    for c in range(N_CORES):
        m = dict(shared)
        m["x"] = np.ascontiguousarray(x[c * nseq:(c + 1) * nseq].reshape(nseq * S, D))
        in_maps.append(m)
    res = run_bass_kernel_spmd(nc, in_maps, core_ids=list(range(N_CORES)))
    out = np.concatenate([np.asarray(r["out"]).reshape(nseq, S, D) for r in res.results], axis=0)
    return out.astype(np.float32)
```

```python
import numpy as np
from contextlib import ExitStack
import concourse.bass as bass
import concourse.mybir as mybir
from concourse.bass_utils import run_bass_kernel_spmd

F32 = mybir.dt.float32
BF16 = mybir.dt.bfloat16
AF = mybir.ActivationFunctionType
ALU = mybir.AluOpType

D = 2048
H = 16
ALPHA = (2.0 * 2) ** 0.25
LN_EPS = 1e-5
NEG = -30000.0
QSCALE = 128 ** -0.5
N_CORES = 8


class Ev:
    __slots__ = ("sem", "val", "eng")

    def __init__(self, sem, val, eng):
        self.sem, self.val, self.eng = sem, val, eng


class Sched:
    ENGS = ("pe", "act", "dve", "pool", "sp")
    DMAQ = ("sp", "pool", "act")

    def __init__(self, nc, stack, ndma=8):
        self.nc = nc
        self.sem = {e: stack.enter_context(nc.semaphore("c_" + e)) for e in self.ENGS}
        self.cnt = dict.fromkeys(self.ENGS, 0)
        self.prog = {e: [] for e in self.ENGS}
        self.known = {e: {} for e in self.ENGS}
        self.dsem = {q: [stack.enter_context(nc.semaphore("d_%s%d" % (q, i))) for i in range(ndma)]
                     for q in self.DMAQ}
        self.dcnt = {q: [0] * ndma for q in self.DMAQ}
        self.drr = dict.fromkeys(self.DMAQ, 0)
        self.tok = {}
        self.defer = {e: ([], []) for e in self.ENGS}
        self.last = dict.fromkeys(self.ENGS, None)
        self.bgsem = [stack.enter_context(nc.semaphore("bg%d" % i)) for i in range(8)]
        self.bgcnt = [0] * 8
        self.bgrr = 0

    def dma_bg(self, q, emit):
        i = self.bgrr
        self.bgrr = (i + 1) % 8
        sem = self.bgsem[i]
        prev = self.bgcnt[i]
        waits = self._need(q, [Ev(sem, prev, "dma")]) if prev else []
        self.bgcnt[i] = prev + 16
        self.prog[q].append((waits, emit, (sem, 16)))

    def _need(self, eng, evs):
        kn = self.known[eng]
        best = {}
        for ev in evs:
            if ev is None:
                continue
            if ev.eng == eng and eng == "pe":
                continue
            k = ev.sem
            if kn.get(k, 0) >= ev.val:
                continue
            if best.get(k, 0) < ev.val:
                best[k] = ev.val
        waits = []
        for k, v in best.items():
            kn[k] = v
            waits.append((k, v))
        return waits

    def _deps(self, reads, writes):
        evs = []
        for t in reads:
            st = self.tok.get(t)
            if st is not None and st[0] is not None:
                evs.append(st[0])
        for t in writes:
            st = self.tok.get(t)
            if st is not None:
                if st[0] is not None:
                    evs.append(st[0])
                evs.extend(st[1].values())
        return evs

    def _commit(self, ev, reads, writes):
        key = ev.sem
        for t in reads:
            st = self.tok.get(t)
            if st is None:
                st = self.tok[t] = [None, {}]
            st[1][key] = ev
        for t in writes:
            self.tok[t] = [ev, {}]

    def op(self, eng, emit, reads=(), writes=(), signal=True):
        evs = self._deps(reads, writes)
        waits = self._need(eng, evs)
        dr, dw = self.defer[eng]
        if signal:
            self.cnt[eng] += 1
            ev = Ev(self.sem[eng], self.cnt[eng], eng)
            self.prog[eng].append((waits, emit, (self.sem[eng], 1)))
            self._commit(ev, list(reads) + dr, list(writes) + dw)
            self.defer[eng] = ([], [])
            self.last[eng] = ev
        else:
            self.prog[eng].append((waits, emit, None))
            dr.extend(reads)
            dw.extend(writes)

    def dma(self, q, emit, reads=(), writes=()):
        evs = self._deps(reads, writes)
        i = self.drr[q]
        self.drr[q] = (i + 1) % len(self.dsem[q])
        sem = self.dsem[q][i]
        prev = self.dcnt[q][i]
        if prev:
            evs.append(Ev(sem, prev, "dma"))
        waits = self._need(q, evs)
        self.dcnt[q][i] = prev + 16
        ev = Ev(sem, prev + 16, "dma")
        self.prog[q].append((waits, emit, (sem, 16)))
        self._commit(ev, reads, writes)

    def barrier(self, bg=False):
        for e in self.ENGS:
            assert not self.defer[e][0] and not self.defer[e][1], e
        evs = [self.last[e] for e in self.ENGS if self.last[e] is not None]
        if bg:
            for i, c in enumerate(self.bgcnt):
                if c:
                    evs.append(Ev(self.bgsem[i], c, "dma"))
        for q in self.DMAQ:
            for i, c in enumerate(self.dcnt[q]):
                if c:
                    evs.append(Ev(self.dsem[q][i], c, "dma"))
        for e in self.ENGS:
            w = self._need(e, evs)
            if w:
                self.prog[e].append((w, None, None))
        self.tok = {}

    def emit_all(self):
        nc = self.nc
        with nc.Block() as block:
            def mk(name):
                def body(e):
                    for waits, emit, inc in self.prog[name]:
                        for s, v in waits:
                            e.wait_ge(s, v)
                        if emit is not None:
                            ins = emit(e)
                            if inc is not None:
                                ins.then_inc(inc[0], inc[1])
                return body
            block.tensor(mk("pe"))
            block.scalar(mk("act"))
            block.vector(mk("dve"))
            block.gpsimd(mk("pool"))
            block.sync(mk("sp"))


def build(NSEQ=2, S=2048, debug=False, stop_phase=99):
    NT = S // 512
    assert NT >= 2
    NG = S // 128
    nc = bass.Bass("TRN2", target_bir_lowering=False)
    dt_in = lambda name, shape, dt=F32: nc.dram_tensor(name, shape, dt, kind="ExternalInput").ap()
    x_d = dt_in("x", [NSEQ * S, D])
    awin = dt_in("awin", [D, 4 * D])
    awout = dt_in("awout", [D, D])
    tab_d = dt_in("tab", [H, 128, 640])
    lwin = dt_in("lwin", [D, 2 * D])
    lwout = dt_in("lwout", [D, D])
    wa_d = dt_in("wa", [16, 128, 128])
    wx_d = dt_in("wx", [16, 128, 128])
    vec_d = dt_in("vecs", [128, 8, 16])
    lng_d = dt_in("lng", [2, D])
    lnb_d = dt_in("lnb", [2, D])
    ident_d = dt_in("ident", [128, 128])
    out_d = nc.dram_tensor("out", [NSEQ * S, D], F32, kind="ExternalOutput").ap()
    skind = "ExternalOutput" if debug else "Internal"
    scr = lambda name, shape, dt: nc.dram_tensor(name, shape, dt, kind=skind).ap()
    QT_d = scr("QT_s", [H, 128, S], BF16)
    KT_d = scr("KT_s", [H, 128, S], BF16)
    V_d = scr("V_s", [S, D], BF16)
    G_d = scr("G_s", [S, D], BF16)
    OGT_d = scr("OGT_s", [D, S], BF16)
    H1_d = scr("H1_s", [S, D], F32)
    YT_d = scr("YT_s", [D, S], BF16)
    WB_d = [nc.dram_tensor("WB%d_s" % i, [D, D], BF16).ap() for i in range(2)]

    with ExitStack() as stack:
        sb = lambda name, shape, dt: stack.enter_context(nc.sbuf_tensor(name, shape, dt))
        actT = sb("actT", [128, 16, S], BF16)
        wbuf = [sb("wbuf%d" % i, [128, 16, 512], BF16) for i in range(2)]
        identf = sb("identf", [128, 128], F32)
        identb = sb("identb", [128, 128], BF16)
        vecs = sb("vecs_sb", [128, 8, 16], F32)
        cl = sb("cl_sb", [128, 4, 16], F32)
        hb = sb("hb_sb", [128, 2, 16], F32)
        wa_sb = sb("wa_sb", [128, 16, 128], BF16)
        wx_sb = sb("wx_sb", [128, 16, 128], BF16)
        PADW = 25088
        pad = sb("pad", [128, PADW], F32)
        pb = [stack.enter_context(nc.psum_tensor("pb%d" % i, [128, 512], F32)) for i in range(8)]
        sc = Sched(nc, stack)

        class PadAlloc:
            def __init__(self):
                self.off = 0

            def f32(self, n):
                a = pad[:, self.off:self.off + n]
                self.off += n
                assert self.off <= PADW, self.off
                return a

            def bf16(self, n):
                assert n % 2 == 0
                a = pad[:, self.off:self.off + n // 2].bitcast(BF16)
                self.off += n // 2
                assert self.off <= PADW, self.off
                return a

        wslot = [0]

        def load_piece(w_ap, col0, ncols=512, dst_col=0):
            s = wslot[0]
            wv = w_ap.rearrange("(kc p) n -> p kc n", p=128)
            toks = []
            for hf in range(2):
                src = wv[:, hf * 8:(hf + 1) * 8, col0:col0 + ncols]
                dst = wbuf[s][:, hf * 8:(hf + 1) * 8, dst_col:dst_col + ncols]
                tk = "wbuf%d_h%d_c%d" % (s, hf, dst_col)
                sc.dma("pool", lambda e, dst=dst, src=src: e.dma_start(out=dst, in_=src),
                       reads=(), writes=(tk,))
                toks.append(tk)
            return tuple(toks)

        def load_piece_bf16(wb_ap, col0):
            s = wslot[0]
            wv = wb_ap.rearrange("(kc p) n -> p kc n", p=128)
            toks = []
            for hf in range(2):
                src = wv[:, hf * 8:(hf + 1) * 8, col0:col0 + 512]
                dst = wbuf[s][:, hf * 8:(hf + 1) * 8, :]
                tk = "wbuf%d_h%d_c0" % (s, hf)
                sc.dma("act", lambda e, dst=dst, src=src: e.dma_start(out=dst, in_=src), writes=(tk,))
                toks.append(tk)
            return tuple(toks)

        def next_slot():
            wslot[0] ^= 1

        sc.dma("sp", lambda e: e.dma_start(out=identf[:], in_=ident_d), writes=("identf",))
        sc.dma("pool", lambda e: e.dma_start(out=identb[:], in_=ident_d), writes=("identb",))
        sc.dma("sp", lambda e: e.dma_start(out=vecs[:], in_=vec_d), writes=("vecs",))
        sc.dma("pool", lambda e: e.dma_start(out=wa_sb[:], in_=wa_d.rearrange("n i j -> i n j")),
               writes=("wa",))
        sc.dma("pool", lambda e: e.dma_start(out=wx_sb[:], in_=wx_d.rearrange("n i j -> i n j")),
               writes=("wx",))
        sc.op("act", lambda e: e.activation(out=cl[:, 2, :], in_=vecs[:, 7, :], func=AF.Exp, scale=-1.0),
              reads=("vecs",), writes=("cl2",))
        sc.op("act", lambda e: e.activation(out=cl[:, 3, :], in_=cl[:, 2, :], func=AF.Ln, bias=1.0),
              reads=("cl2",), writes=("cl3",))
        sc.op("dve", lambda e: e.tensor_scalar(cl[:, 0, :], cl[:, 3, :], -8.0, None, ALU.mult),
              reads=("cl3",), writes=("cl0",))
        sc.op("dve", lambda e: e.tensor_scalar(cl[:, 1, :], cl[:, 3, :], -4.0, None, ALU.mult),
              reads=("cl3",), writes=("cl1",))
        sc.op("dve", lambda e: e.tensor_scalar(hb[:, :, :], vecs[:, 5:7, :], 0.5, None, ALU.mult),
              reads=("vecs",), writes=("hb",))
        sc.barrier()

        def transposes_f32(src, src_tok, g, bank0, evac_flip, all_act=False, tok=False):
            for b in range(4):
                bank = bank0 + (b % 2)
                for kk in range(4):
                    o = pb[bank][:, kk * 128:(kk + 1) * 128]
                    i_ = src[:, (4 * b + kk) * 128:(4 * b + kk + 1) * 128]
                    sc.op("pe", lambda e, o=o, i_=i_: e.transpose(o, i_, identf[:]),
                          reads=(src_tok, "identf"), writes=("pb%d" % bank,), signal=(kk == 3))
                dst = actT[:, 4 * b:4 * b + 4, g * 128:(g + 1) * 128]
                srcp = pb[bank][:, :].rearrange("p (k t) -> p k t", k=4)
                if all_act or (b + evac_flip) % 2 == 0:
                    sc.op("act", lambda e, dst=dst, srcp=srcp: e.activation(out=dst, in_=srcp, func=AF.Copy),
                          reads=("pb%d" % bank,), writes=(("aT%d_%d" % (g, b),) if tok else ()))
                else:
                    sc.op("dve", lambda e, dst=dst, srcp=srcp: e.tensor_copy(dst, srcp),
                          reads=("pb%d" % bank,), writes=(("aT%d_%d" % (g, b),) if tok else ()))

        def transposes_b16(src, src_tok, g, bank0, evac_flip):
            for b in range(4):
                bank = bank0 + (b % 2)
                pv = pb[bank][:, 0:256].bitcast(BF16)
                for kk in range(4):
                    o = pv[:, kk * 128:(kk + 1) * 128]
                    i_ = src[:, (4 * b + kk) * 128:(4 * b + kk + 1) * 128]
                    sc.op("pe", lambda e, o=o, i_=i_: e.transpose(o, i_, identb[:]),
                          reads=(src_tok, "identb"), writes=("pb%d" % bank,), signal=(kk == 3))
                dst = actT[:, 4 * b:4 * b + 4, g * 128:(g + 1) * 128]
                srcp = pv.rearrange("p (k t) -> p k t", k=4)
                if (b + evac_flip) % 2 == 0:
                    sc.op("act", lambda e, dst=dst, srcp=srcp: e.activation(out=dst, in_=srcp, func=AF.Copy),
                          reads=("pb%d" % bank,), writes=())
                else:
                    sc.op("dve", lambda e, dst=dst, srcp=srcp: e.tensor_copy(dst, srcp),
                          reads=("pb%d" % bank,), writes=())

        for sq in range(NSEQ):
            row0 = sq * S
            pa = PadAlloc()
            xs = [pa.f32(2048) for _ in range(4)]
            stg = [pa.bf16(4 * 512) for _ in range(2)]
            for g in range(NG):
                xb = xs[g % 4]
                src = x_d[row0 + g * 128: row0 + (g + 1) * 128, :]
                sc.dma("sp", lambda e, xb=xb, src=src: e.dma_start(out=xb, in_=src),
                       writes=("xs%d" % (g % 4),))
                transposes_f32(xb, "xs%d" % (g % 4), g, 2 * (g % 4), g, tok=True)
            if stop_phase <= 0:
                break
            nstage = [0]
            for p in range(16):
                s = wslot[0]
                wtoks = load_piece(awin, p * 512)
                if sq == 0 and p < 8:
                    wsrc = (awout if p < 4 else lwout)[:, (p % 4) * 512:(p % 4 + 1) * 512]
                    wdst = WB_d[p // 4][:, (p % 4) * 512:(p % 4 + 1) * 512]
                    sc.dma_bg("pool", lambda e, wdst=wdst, wsrc=wsrc: e.dma_start(out=wdst, in_=wsrc))
                kind_p = p // 4
                hp = p % 4
                for t in range(NT):
                    si = nstage[0] % 2
                    nstage[0] += 1
                    stv = stg[si].rearrange("p (j n) -> p j n", j=4)
                    stok = "stg%d" % si
                    for j in range(4):
                        bank = (4 * (t % 2) + j)
                        btok = "pb%d" % bank
                        for kc in range(16):
                            if kind_p < 2:
                                lhsT = wbuf[s][:, kc, j * 128:(j + 1) * 128]
                                rhs = actT[:, kc, t * 512:(t + 1) * 512]
                                atoks = tuple("aT%d_%d" % (4 * t + gg_, kc // 4) for gg_ in range(4))
                            else:
                                lhsT = actT[:, kc, t * 512 + j * 128: t * 512 + (j + 1) * 128]
                                rhs = wbuf[s][:, kc, :]
                                atoks = ("aT%d_%d" % (4 * t + j, kc // 4),)
                            sc.op("pe", lambda e, o=pb[bank][:, :], l=lhsT, r=rhs, kc=kc:
                                  e.matmul(o, l, r, start=(kc == 0), stop=(kc == 15)),
                                  reads=wtoks + (atoks if p == 0 else ()), writes=(btok,), signal=(kc == 15))
                        o = stv[:, j, :]
                        if kind_p == 0:
                            sc.op("act", lambda e, o=o, i_=pb[bank][:, :]: e.activation(
                                out=o, in_=i_, func=AF.Copy, scale=QSCALE),
                                reads=(btok,), writes=(stok,))
                        elif kind_p == 3:
                            sc.op("act", lambda e, o=o, i_=pb[bank][:, :]: e.activation(
                                out=o, in_=i_, func=AF.Silu), reads=(btok,), writes=(stok,))
                        else:
                            sc.op("dve", lambda e, o=o, i_=pb[bank][:, :]: e.tensor_copy(o, i_),
                                  reads=(btok,), writes=(stok,))
                    if kind_p < 2:
                        dd = (QT_d if kind_p == 0 else KT_d)[4 * hp:4 * hp + 4, :, t * 512:(t + 1) * 512]
                        dd = dd.rearrange("j p n -> p j n")
                    else:
                        dd = (V_d if kind_p == 2 else G_d)[t * 512:(t + 1) * 512, hp * 512:(hp + 1) * 512]
                        dd = dd.rearrange("(j p) n -> p j n", p=128)
                    sc.dma("sp", lambda e, dd=dd, stv=stv: e.dma_start(out=dd, in_=stv),
                           reads=(stok,), writes=())
                next_slot()
            sc.barrier()
            if stop_phase <= 1:
                continue

            pa = PadAlloc()
            ktb = [pa.bf16(S) for _ in range(2)]
            qtb = [pa.bf16(S) for _ in range(2)]
            vb = [pa.bf16(NG * 132).rearrange("p (t d) -> p t d", d=132) for _ in range(2)]
            gb = [pa.bf16(NG * 128).rearrange("p (t d) -> p t d", d=128) for _ in range(2)]
            tabb = [pa.bf16(640) for _ in range(2)]
            NSS = 5
            ssb = [pa.f32(512) for _ in range(NSS)]
            pTb = [pa.bf16(2560) for _ in range(2)]
            ogtok = [pa.bf16(128) for _ in range(8)]
            rcb = [pa.f32(2) for _ in range(8)]
            ogTs = [pa.bf16(S) for _ in range(2)]
            STB = (0, 1, 2, 5, 6)
            import os
            PREF = int(os.environ.get("PREF", "1"))
            NSTB = int(os.environ.get("NSTB", "5"))
            for i in range(2):
                sc.op("pool", lambda e, a=vb[i][:, :, 128:129]: e.memset(a, 1.0), writes=("v%d" % i,))

            def head_loads(h):
                hb = h % 2
                sc.dma("sp", lambda e, o=ktb[hb], i_=KT_d[h]: e.dma_start(out=o, in_=i_), writes=("kt%d" % hb,))
                sc.dma("sp", lambda e, o=qtb[hb], i_=QT_d[h]: e.dma_start(out=o, in_=i_), writes=("qt%d" % hb,))
                sc.dma("sp", lambda e, o=vb[hb][:, :, 0:128],
                       i_=V_d[:, h * 128:(h + 1) * 128].rearrange("(t p) d -> p t d", p=128):
                       e.dma_start(out=o, in_=i_), writes=("v%d" % hb,))
                sc.dma("sp", lambda e, o=gb[hb],
                       i_=G_d[:, h * 128:(h + 1) * 128].rearrange("(t p) d -> p t d", p=128):
                       e.dma_start(out=o, in_=i_), writes=("g%d" % hb,))
                sc.dma("pool", lambda e, o=tabb[hb], i_=tab_d[h]: e.dma_start(out=o, in_=i_), writes=("tab%d" % hb,))
                sc.op("pool", lambda e, a=tabb[hb][0:64, 576:640]: e.memset(a, NEG), writes=("tab%d" % hb,))
                sc.op("pool", lambda e, a=tabb[hb][64:128, 0:64]: e.memset(a, NEG), writes=("tab%d" % hb,))

            items2 = [(h, qi) for h in range(H) for qi in range(NT)]
            infos = {}
            nss = [0]

            def p2_f1(n):
                h, qi = items2[n]
                hb = h % 2
                if PREF and qi == 1 and h + 1 < H:
                    head_loads(h + 1)
                if (not PREF) and qi == 0 and h > 0:
                    head_loads(h)
                q0 = qi * 512
                pi = n % 2
                ptok = "pT%d" % pi
                info = {}
                off = 0
                for j in range(8):
                    J = 4 * qi - 4 + j
                    if J < 0:
                        continue
                    qlo = max(0, 2 * j - 8)
                    qhi = min(7, 2 * j + 1)
                    w = 64 * (qhi - qlo + 1)
                    c0 = 64 * (qlo - 2 * j + 8)
                    info[j] = (J, qlo, w, off)
                    k_ = nss[0]
                    nss[0] += 1
                    bank = STB[k_ % NSTB]
                    si = k_ % NSS
                    sc.op("pe", lambda e, o=pb[bank][:, 0:w], l=ktb[hb][:, J * 128:(J + 1) * 128],
                          r=qtb[hb][:, q0 + 64 * qlo: q0 + 64 * qlo + w]:
                          e.matmul(o, l, r, start=True, stop=False),
                          reads=("kt%d" % hb, "qt%d" % hb), writes=("pb%d" % bank,), signal=False)
                    sc.op("pe", lambda e, o=pb[bank][:, 0:w], l=identb[:], r=tabb[hb][:, c0:c0 + w]:
                          e.matmul(o, l, r, start=False, stop=True),
                          reads=("identb", "tab%d" % hb), writes=("pb%d" % bank,))
                    sc.op("act", lambda e, o=pTb[pi][:, off:off + w], a=pb[bank][:, 0:w]:
                          e.activation(out=o, in_=a, func=AF.Exp),
                          reads=("pb%d" % bank,), writes=(ptok,))
                    off += w
                infos[n] = info

            def p2_f2(n):
                h, qi = items2[n]
                hb = h % 2
                pi = n % 2
                ptok = "pT%d" % pi
                info = infos.pop(n)
                for g in range(4):
                    js = [j for j in range(g, g + 5) if j in info]
                    obank = 3 + (g % 2)
                    ocol = 0
                    otok = "pb%d" % obank
                    oap = pb[obank][:, ocol:ocol + 129]
                    for idx, j in enumerate(js):
                        J, qlo, w, poff = info[j]
                        c = poff + 64 * (2 * g - qlo)
                        sc.op("pe", lambda e, o=oap, l=pTb[pi][:, c:c + 128], r=vb[hb][:, J, 0:129],
                              st=(idx == 0), sp_=(idx == len(js) - 1):
                              e.matmul(o, l, r, start=st, stop=sp_),
                              reads=(ptok, "v%d" % hb), writes=(otok,), signal=(idx == len(js) - 1))
                    oi = (n % 2) * 4 + g
                    sc.op("dve", lambda e, o=rcb[oi][:, 0:1], a=pb[obank][:, ocol + 128:ocol + 129]:
                          e.reciprocal(o, a), reads=(otok,), writes=("rc%d" % oi,))
                    sc.op("dve", lambda e, o=ogtok[oi], a=pb[obank][:, ocol:ocol + 128], s_=rcb[oi][:, 0:1],
                          b=gb[hb][:, 4 * qi + g, :]:
                          e.scalar_tensor_tensor(o, a, s_, b, ALU.mult, ALU.mult),
                          reads=(otok, "rc%d" % oi, "g%d" % hb), writes=("ogtok%d" % oi,))

            def p2_f3(n):
                h, qi = items2[n]
                hb = h % 2
                q0 = qi * 512
                for g in range(4):
                    oi = (n % 2) * 4 + g
                    tpo = pb[7][:, 0:256].bitcast(BF16)[:, g * 128:(g + 1) * 128]
                    sc.op("pe", lambda e, o=tpo, a=ogtok[oi]: e.transpose(o, a, identb[:]),
                          reads=("ogtok%d" % oi, "identb"), writes=("pb7",), signal=(g == 3))
                sc.op("act", lambda e, o=ogTs[hb][:, q0:q0 + 512], a=pb[7][:, 0:256].bitcast(BF16):
                      e.activation(out=o, in_=a, func=AF.Copy),
                      reads=("pb7",), writes=("ogTs%d" % hb,))
                if qi == NT - 1:
                    sc.dma("sp", lambda e, o=OGT_d[h * 128:(h + 1) * 128, :], a=ogTs[hb]:
                           e.dma_start(out=o, in_=a), reads=("ogTs%d" % hb,))

            head_loads(0)
            NI = len(items2)
            import os
            SK1 = int(os.environ.get("SK1", "1"))
            SK2 = int(os.environ.get("SK2", "2"))
            for step in range(NI + SK2):
                if step < NI:
                    p2_f1(step)
                if 0 <= step - SK1 < NI:
                    p2_f2(step - SK1)
                if 0 <= step - SK2 < NI:
                    p2_f3(step - SK2)
            sc.barrier(bg=True)
            if stop_phase <= 2:
                continue

            def outproj_ln(srcT_d, w_d, layer, resid_d, resid_row0, dst_d, dst_row0, do_T):
                pa = PadAlloc()
                ogt = pa.bf16(16 * 512).rearrange("p (k n) -> p k n", k=16)
                xz = [[pa.f32(2048) for _ in range(4)] for _ in range(2)]
                gbc = pa.f32(2048)
                bbc = pa.f32(2048)
                stats = [pa.f32(24) for _ in range(2)]
                mv = [pa.f32(8) for _ in range(2)]
                nbk = [0]
                sc.dma("sp", lambda e: e.dma_start(out=gbc, in_=lng_d[layer, :].partition_broadcast(128)),
                       writes=("gbc",))
                sc.dma("sp", lambda e: e.dma_start(out=bbc, in_=lnb_d[layer, :].partition_broadcast(128)),
                       writes=("bbc",))
                sv = srcT_d.rearrange("(kc p) n -> p kc n", p=128)
                xtok = lambda t, i: "xz%d_%d" % (t % 2, i)

                ogtb = None if do_T else [actT[:, :, 0:512], actT[:, :, 512:1024]]

                def ogt_of(t):
                    return ogt if do_T else ogtb[t % 2]

                def ogt_toks(t, i):
                    return ("ogtq%d" % i,) if do_T else ("ogtb%dh0" % (t % 2), "ogtb%dh1" % (t % 2))

                def ogt_load(t):
                    if do_T:
                        for i in range(4):
                            sc.dma("sp", lambda e, o=ogt[:, :, i * 128:(i + 1) * 128],
                                   i_=sv[:, :, t * 512 + i * 128: t * 512 + (i + 1) * 128]:
                                   e.dma_start(out=o, in_=i_), writes=("ogtq%d" % i,))
                    else:
                        for hf in range(2):
                            sc.dma("sp", lambda e, o=ogtb[t % 2][:, hf * 8:(hf + 1) * 8, :],
                                   i_=sv[:, hf * 8:(hf + 1) * 8, t * 512:(t + 1) * 512]:
                                   e.dma_start(out=o, in_=i_), writes=("ogtb%dh%d" % (t % 2, hf),))

                def xz_loads(t):
                    for i in range(4):
                        r0 = resid_row0 + t * 512 + i * 128
                        sc.dma("sp", lambda e, o=xz[t % 2][i], i_=resid_d[r0:r0 + 128, :]:
                               e.dma_start(out=o, in_=i_), writes=(xtok(t, i),))

                wq = []

                def prefetch(n):
                    s_ = wslot[0]
                    wq.append((s_, load_piece_bf16(w_d, n * 512)))
                    next_slot()

                def mm_piece(t, n):
                    s, wtoks = wq.pop(0)
                    for i in range(4):
                        bank = nbk[0] % 6
                        nbk[0] += 1
                        btok = "pb%d" % bank
                        for kc in range(16):
                            sc.op("pe", lambda e, o=pb[bank][:, :], l=ogt_of(t)[:, kc, i * 128:(i + 1) * 128],
                                  r=wbuf[s][:, kc, :], kc=kc:
                                  e.matmul(o, l, r, start=(kc == 0), stop=(kc == 15)),
                                  reads=wtoks + ogt_toks(t, i), writes=(btok,), signal=(kc == 15))
                        zz = xz[t % 2][i][:, n * 512:(n + 1) * 512]
                        sc.op("dve", lambda e, zz=zz, a=pb[bank][:, :]:
                              e.scalar_tensor_tensor(zz, zz, ALPHA, a, ALU.mult, ALU.add),
                              reads=(btok, xtok(t, i)), writes=(xtok(t, i),))
                    if 4 * t + n + 2 < 4 * NT:
                        prefetch((n + 2) % 4)

                def ln_group(t, i):
                    si = i % 2
                    xt = xtok(t, i)
                    z = xz[t % 2][i]
                    for c4 in range(4):
                        sc.op("dve", lambda e, o=stats[si][:, c4 * 6:(c4 + 1) * 6],
                              a=z[:, c4 * 512:(c4 + 1) * 512]: e.bn_stats(o, a),
                              reads=(xt,), writes=("stats%d" % si,))
                    sc.op("dve", lambda e, o=mv[si][:, 0:2], a=stats[si]: e.bn_aggr(o, a),
                          reads=("stats%d" % si,), writes=("mv%d" % si,))
                    sc.op("dve", lambda e, o=mv[si][:, 2:3], a=mv[si][:, 1:2]:
                          e.tensor_scalar(o, a, LN_EPS, None, ALU.add),
                          reads=("mv%d" % si,), writes=("mvb%d" % si,))
                    sc.op("act", lambda e, o=mv[si][:, 3:4], a=mv[si][:, 2:3]:
                          e.activation(out=o, in_=a, func=AF.Sqrt),
                          reads=("mvb%d" % si,), writes=("mvc%d" % si,))
                    sc.op("dve", lambda e, o=mv[si][:, 4:5], a=mv[si][:, 3:4]: e.reciprocal(o, a),
                          reads=("mvc%d" % si,), writes=("mvd%d" % si,))
                    sc.op("dve", lambda e, z=z, m=mv[si][:, 0:1]:
                          e.scalar_tensor_tensor(z, z, m, gbc, ALU.subtract, ALU.mult),
                          reads=(xt, "mv%d" % si, "gbc"), writes=(xt,))
                    sc.op("act", lambda e, z=z, r=mv[si][:, 4:5]:
                          e.activation(out=z, in_=z, func=AF.Identity, scale=r),
                          reads=(xt, "mvd%d" % si), writes=(xt,))
                    sc.op("pool", lambda e, z=z: e.tensor_tensor(z, z, bbc, ALU.add),
                          reads=(xt, "bbc"), writes=(xt,))
                    r0 = dst_row0 + t * 512 + i * 128
                    sc.dma("sp", lambda e, o=dst_d[r0:r0 + 128, :], a=z: e.dma_start(out=o, in_=a),
                           reads=(xt,))

                def tr_group(t, i):
                    transposes_f32(xz[t % 2][i], xtok(t, i), 4 * t + i, 6, 0, all_act=True)

                ogt_load(0)
                xz_loads(0)
                prefetch(0)
                prefetch(1)
                for t in range(NT):
                    mm_piece(t, 0)
                    if (not do_T) and t + 1 < NT:
                        ogt_load(t + 1)
                    if t > 0:
                        ln_group(t - 1, 0)
                        ln_group(t - 1, 1)
                    mm_piece(t, 1)
                    if t > 0:
                        ln_group(t - 1, 2)
                        ln_group(t - 1, 3)
                    mm_piece(t, 2)
                    if t > 0 and do_T:
                        for i in range(4):
                            tr_group(t - 1, i)
                    if t + 1 < NT:
                        xz_loads(t + 1)
                    mm_piece(t, 3)
                    if do_T and t + 1 < NT:
                        ogt_load(t + 1)
                for i in range(4):
                    ln_group(NT - 1, i)
                    if do_T:
                        tr_group(NT - 1, i)
                sc.barrier()

            outproj_ln(OGT_d, WB_d[0], 0, x_d, row0, H1_d, 0, True)
            if stop_phase <= 3:
                continue

            pa = PadAlloc()
            upad = [pa.f32(S + 4) for _ in range(2)]
            ysb = [pa.bf16(S) for _ in range(2)]
            NTB = 3
            uc = [pa.f32(512) for _ in range(NTB)]
            ucb = [pa.bf16(512) for _ in range(NTB)]
            rr = [pa.f32(512) for _ in range(NTB)]
            ii = [pa.f32(512) for _ in range(NTB)]
            aa = [pa.f32(512) for _ in range(NTB)]
            mm = [pa.f32(512) for _ in range(NTB)]
            bb = [pa.f32(512) for _ in range(NTB)]
            hh = [pa.f32(512) for _ in range(NTB)]
            sg = [pa.f32(512) for _ in range(NTB)]
            gg = [pa.f32(512) for _ in range(NTB)]
            for i in range(2):
                sc.op("pool", lambda e, a=upad[i][:, 0:3]: e.memset(a, 0.0), writes=("upad%d" % i,))
            wt4 = {}

            def p4_wload(c2):
                wt4[c2] = (wslot[0],
                           load_piece(lwin, c2 * 256, 256, 0) + load_piece(lwin, D + c2 * 256, 256, 256))
                next_slot()

            NI4 = 16 * NT

            def p4_ids(n):
                c, t = n // NT, n % NT
                return c, t, c % 2, n % NTB, "_%d" % (n % NTB)

            def p4_f1a(n):
                c, t, ub, tb, T = p4_ids(n)
                c2, jj = c // 2, c % 2
                if jj == 0 and t == 0 and c2 + 1 < 8:
                    p4_wload(c2 + 1)
                s, wtoks = wt4[c2]
                bu, bg = 2 * (n % 2), 2 * (n % 2) + 1
                for (bank, colb) in ((bu, jj * 128), (bg, 256 + jj * 128)):
                    for kc in range(16):
                        sc.op("pe", lambda e, o=pb[bank][:, :], l=wbuf[s][:, kc, colb:colb + 128],
                              r=actT[:, kc, t * 512:(t + 1) * 512], kc=kc:
                              e.matmul(o, l, r, start=(kc == 0), stop=(kc == 15)),
                              reads=wtoks, writes=("pb%d" % bank,), signal=(kc == 15))

            def p4_f1b(n):
                c, t, ub, tb, T = p4_ids(n)
                bu, bg = 2 * (n % 2), 2 * (n % 2) + 1
                utok = "upad%d" % ub
                sc.op("act", lambda e, o=upad[ub][:, 3 + t * 512: 3 + (t + 1) * 512], a=pb[bu][:, :]:
                      e.activation(out=o, in_=a, func=AF.Copy),
                      reads=("pb%d" % bu,), writes=(utok + "_t%d" % t,))
                sc.op("act", lambda e, o=sg[tb], a=pb[bg][:, :]: e.activation(out=o, in_=a, func=AF.Tanh, scale=0.5),
                      reads=("pb%d" % bg,), writes=("sg" + T,))
                sc.op("dve", lambda e, o=gg[tb], a=pb[bg][:, :]: e.tensor_copy(o, a),
                      reads=("pb%d" % bg, "sg" + T), writes=("gg" + T,))

            def p4_f2(n):
                c, t, ub, tb, T = p4_ids(n)
                utok = "upad%d" % ub
                br, bi = 4 + 2 * (n % 2), 5 + 2 * (n % 2)
                ureads = (utok, utok + "_t%d" % t) + ((utok + "_t%d" % (t - 1),) if t > 0 else ())
                sc.op("dve", lambda e, o=uc[tb], a=upad[ub][:, t * 512: t * 512 + 512],
                      w0=vecs[:, 0, c:c + 1], cb=vecs[:, 4, c:c + 1]:
                      e.tensor_scalar(o, a, w0, cb, ALU.mult, ALU.add),
                      reads=ureads + ("vecs",), writes=("uc" + T,))
                for k in range(1, 4):
                    sc.op("dve", lambda e, o=uc[tb], a=upad[ub][:, t * 512 + k: t * 512 + k + 512],
                          wk=vecs[:, k, c:c + 1]:
                          e.scalar_tensor_tensor(o, a, wk, o, ALU.mult, ALU.add),
                          reads=ureads + ("vecs", "uc" + T), writes=("uc" + T,))
                sc.op("act", lambda e, o=ucb[tb], a=uc[tb]: e.activation(out=o, in_=a, func=AF.Copy),
                      reads=("uc" + T,), writes=("ucb" + T,))
                sc.op("pe", lambda e, o=pb[br][:, :], l=wa_sb[:, c, :], r=ucb[tb]:
                      e.matmul(o, l, r, start=True, stop=True),
                      reads=("wa", "ucb" + T), writes=("pb%d" % br,))
                sc.op("pe", lambda e, o=pb[bi][:, :], l=wx_sb[:, c, :], r=ucb[tb]:
                      e.matmul(o, l, r, start=True, stop=True),
                      reads=("wx", "ucb" + T), writes=("pb%d" % bi,))

            def p4_f3a(n):
                c, t, ub, tb, T = p4_ids(n)
                br, bi = 4 + 2 * (n % 2), 5 + 2 * (n % 2)
                sc.op("act", lambda e, o=rr[tb], a=pb[br][:, :], b=hb[:, 0, c:c + 1]:
                      e.activation(out=o, in_=a, func=AF.Tanh, bias=b, scale=0.5),
                      reads=("pb%d" % br, "hb"), writes=("rr" + T,))
                sc.op("act", lambda e, o=ii[tb], a=pb[bi][:, :], b=hb[:, 1, c:c + 1]:
                      e.activation(out=o, in_=a, func=AF.Tanh, bias=b, scale=0.5),
                      reads=("pb%d" % bi, "hb"), writes=("ii" + T,))
                sc.op("act", lambda e, o=aa[tb], a=rr[tb], s_=cl[:, 1, c:c + 1]:
                      e.activation(out=o, in_=a, func=AF.Exp, scale=s_, bias=s_),
                      reads=("rr" + T, "cl1"), writes=("aa" + T,))
                sc.op("act", lambda e, o=mm[tb], a=rr[tb], s_=cl[:, 0, c:c + 1]:
                      e.activation(out=o, in_=a, func=AF.Exp, scale=s_, bias=s_),
                      reads=("rr" + T, "cl0"), writes=("mm" + T,))
                sc.op("dve", lambda e, o=bb[tb], a=ii[tb], b=uc[tb]:
                      e.scalar_tensor_tensor(o, a, 1.0, b, ALU.add, ALU.mult),
                      reads=("ii" + T, "uc" + T), writes=("bb" + T,))
                sc.op("pool", lambda e, o=mm[tb]: e.tensor_scalar(o, o, 1.0, 0.0, ALU.min, ALU.max),
                      reads=("mm" + T,), writes=("mm" + T,))
                sc.op("act", lambda e, o=mm[tb]: e.activation(out=o, in_=o, func=AF.Sqrt, scale=-0.25, bias=0.25),
                      reads=("mm" + T,), writes=("mm" + T,))
                sc.op("pool", lambda e, o=bb[tb], a=mm[tb]: e.tensor_tensor(o, o, a, ALU.mult),
                      reads=("bb" + T, "mm" + T), writes=("bb" + T,))
                sc.op("pool", lambda e, o=sg[tb]: e.tensor_scalar(o, o, 0.5, 0.5, ALU.mult, ALU.add),
                      reads=("sg" + T,), writes=("sg" + T,))
                sc.op("pool", lambda e, o=sg[tb], a=gg[tb]: e.tensor_tensor(o, o, a, ALU.mult),
                      reads=("sg" + T, "gg" + T), writes=("sg" + T,))

            def p4_f3b(n):
                c, t, ub, tb, T = p4_ids(n)
                ptb = (n - 1) % NTB
                init = 0.0 if t == 0 else hh[ptb][:, 511:512]
                sc.op("dve", lambda e, o=hh[tb], a=aa[tb], b=bb[tb], init=init:
                      e.tensor_tensor_scan(o, a, b, init, ALU.mult, ALU.add),
                      reads=("aa" + T, "bb" + T) + (("hh_%d" % ptb,) if t > 0 else ()),
                      writes=("hh" + T,))
                sc.op("pool", lambda e, o=ysb[ub][:, t * 512:(t + 1) * 512], a=hh[tb], b=sg[tb]:
                      e.tensor_tensor(o, a, b, ALU.mult),
                      reads=("hh" + T, "sg" + T), writes=("ysb%d" % ub,))
                if t == NT - 1:
                    sc.dma("sp", lambda e, o=YT_d[c * 128:(c + 1) * 128, :], a=ysb[ub]: e.dma_start(out=o, in_=a),
                           reads=("ysb%d" % ub,))

            p4_wload(0)
            for step in range(NI4 + 2):
                if step < NI4:
                    p4_f1a(step)
                if 0 <= step - 2 < NI4:
                    p4_f3a(step - 2)
                if 0 <= step - 1 < NI4:
                    p4_f2(step - 1)
                if 0 <= step - 2 < NI4:
                    p4_f3b(step - 2)
                if step < NI4:
                    p4_f1b(step)
            sc.barrier()
            if stop_phase <= 4:
                continue
            outproj_ln(YT_d, WB_d[1], 1, H1_d, 0, out_d, row0, False)

        sc.barrier()
        sc.emit_all()
    return nc


def _prep_shared(inp):
    f = lambda a: np.ascontiguousarray(np.asarray(a, dtype=np.float32))
    k = np.arange(128)[:, None]
    xq = np.arange(640)[None, :]
    idx = np.minimum(xq - k, 256) + 256
    tab = f(np.asarray(inp["attn_rel_bias"])[0][:, idx])
    cw = np.asarray(inp["lru_conv_w"])[0]
    vec = np.stack([cw[0], cw[1], cw[2], cw[3], np.asarray(inp["lru_conv_b"])[0],
                    np.asarray(inp["lru_ba"])[0].reshape(-1), np.asarray(inp["lru_bx"])[0].reshape(-1),
                    np.asarray(inp["lru_lambda"])[0]], axis=0)
    vecs = f(vec.reshape(8, 16, 128).transpose(2, 0, 1))
    return {
        "awin": f(np.asarray(inp["attn_w_in"])[0]), "awout": f(np.asarray(inp["attn_w_out"])[0]),
        "tab": tab, "lwin": f(np.asarray(inp["lru_w_in"])[0]), "lwout": f(np.asarray(inp["lru_w_out"])[0]),
        "wa": f(np.asarray(inp["lru_wa"])[0]), "wx": f(np.asarray(inp["lru_wx"])[0]), "vecs": vecs,
        "lng": f(inp["ln_gain"]), "lnb": f(inp["ln_bias"]), "ident": np.eye(128, dtype=np.float32),
    }


def kernel(**inputs):
    x = np.asarray(inputs["x"], dtype=np.float32)
    B, S, _ = x.shape
    nseq = B // N_CORES
    shared = _prep_shared(inputs)
    nc = build(NSEQ=nseq, S=S)
    in_maps = []
    for c in range(N_CORES):
        m = dict(shared)
        m["x"] = np.ascontiguousarray(x[c * nseq:(c + 1) * nseq].reshape(nseq * S, D))
        in_maps.append(m)
    res = run_bass_kernel_spmd(nc, in_maps, core_ids=list(range(N_CORES)))
    out = np.concatenate([np.asarray(r["out"]).reshape(nseq, S, D) for r in res.results], axis=0)
    return out.astype(np.float32)
```

```python
import numpy as np
from contextlib import ExitStack
import concourse.bass as bass
import concourse.mybir as mybir
from concourse.bass_utils import run_bass_kernel_spmd

F32 = mybir.dt.float32
BF16 = mybir.dt.bfloat16
AF = mybir.ActivationFunctionType
ALU = mybir.AluOpType

D = 2048
H = 16
ALPHA = (2.0 * 2) ** 0.25
LN_EPS = 1e-5
NEG = -30000.0
QSCALE = 128 ** -0.5
N_CORES = 8


class Ev:
    __slots__ = ("sem", "val", "eng")

    def __init__(self, sem, val, eng):
        self.sem, self.val, self.eng = sem, val, eng


class Sched:
    ENGS = ("pe", "act", "dve", "pool", "sp")
    DMAQ = ("sp", "pool", "act")

    def __init__(self, nc, stack, ndma=8):
        self.nc = nc
        self.sem = {e: stack.enter_context(nc.semaphore("c_" + e)) for e in self.ENGS}
        self.cnt = dict.fromkeys(self.ENGS, 0)
        self.prog = {e: [] for e in self.ENGS}
        self.known = {e: {} for e in self.ENGS}
        self.dsem = {q: [stack.enter_context(nc.semaphore("d_%s%d" % (q, i))) for i in range(ndma)]
                     for q in self.DMAQ}
        self.dcnt = {q: [0] * ndma for q in self.DMAQ}
        self.drr = dict.fromkeys(self.DMAQ, 0)
        self.tok = {}
        self.defer = {e: ([], []) for e in self.ENGS}
        self.last = dict.fromkeys(self.ENGS, None)
        self.bgsem = [stack.enter_context(nc.semaphore("bg%d" % i)) for i in range(8)]
        self.bgcnt = [0] * 8
        self.bgrr = 0

    def dma_bg(self, q, emit):
        i = self.bgrr
        self.bgrr = (i + 1) % 8
        sem = self.bgsem[i]
        prev = self.bgcnt[i]
        waits = self._need(q, [Ev(sem, prev, "dma")]) if prev else []
        self.bgcnt[i] = prev + 16
        self.prog[q].append((waits, emit, (sem, 16)))

    def _need(self, eng, evs):
        kn = self.known[eng]
        best = {}
        for ev in evs:
            if ev is None:
                continue
            if ev.eng == eng and eng == "pe":
                continue
            k = ev.sem
            if kn.get(k, 0) >= ev.val:
                continue
            if best.get(k, 0) < ev.val:
                best[k] = ev.val
        waits = []
        for k, v in best.items():
            kn[k] = v
            waits.append((k, v))
        return waits

    def _deps(self, reads, writes):
        evs = []
        for t in reads:
            st = self.tok.get(t)
            if st is not None and st[0] is not None:
                evs.append(st[0])
        for t in writes:
            st = self.tok.get(t)
            if st is not None:
                if st[0] is not None:
                    evs.append(st[0])
                evs.extend(st[1].values())
        return evs

    def _commit(self, ev, reads, writes):
        key = ev.sem
        for t in reads:
            st = self.tok.get(t)
            if st is None:
                st = self.tok[t] = [None, {}]
            st[1][key] = ev
        for t in writes:
            self.tok[t] = [ev, {}]

    def op(self, eng, emit, reads=(), writes=(), signal=True):
        evs = self._deps(reads, writes)
        waits = self._need(eng, evs)
        dr, dw = self.defer[eng]
        if signal:
            self.cnt[eng] += 1
            ev = Ev(self.sem[eng], self.cnt[eng], eng)
            self.prog[eng].append((waits, emit, (self.sem[eng], 1)))
            self._commit(ev, list(reads) + dr, list(writes) + dw)
            self.defer[eng] = ([], [])
            self.last[eng] = ev
        else:
            self.prog[eng].append((waits, emit, None))
            dr.extend(reads)
            dw.extend(writes)

    def dma(self, q, emit, reads=(), writes=()):
        evs = self._deps(reads, writes)
        i = self.drr[q]
        self.drr[q] = (i + 1) % len(self.dsem[q])
        sem = self.dsem[q][i]
        prev = self.dcnt[q][i]
        if prev:
            evs.append(Ev(sem, prev, "dma"))
        waits = self._need(q, evs)
        self.dcnt[q][i] = prev + 16
        ev = Ev(sem, prev + 16, "dma")
        self.prog[q].append((waits, emit, (sem, 16)))
        self._commit(ev, reads, writes)

    def barrier(self, bg=False):
        for e in self.ENGS:
            assert not self.defer[e][0] and not self.defer[e][1], e
        evs = [self.last[e] for e in self.ENGS if self.last[e] is not None]
        if bg:
            for i, c in enumerate(self.bgcnt):
                if c:
                    evs.append(Ev(self.bgsem[i], c, "dma"))
        for q in self.DMAQ:
            for i, c in enumerate(self.dcnt[q]):
                if c:
                    evs.append(Ev(self.dsem[q][i], c, "dma"))
        for e in self.ENGS:
            w = self._need(e, evs)
            if w:
                self.prog[e].append((w, None, None))
        self.tok = {}

    def emit_all(self):
        nc = self.nc
        with nc.Block() as block:
            def mk(name):
                def body(e):
                    for waits, emit, inc in self.prog[name]:
                        for s, v in waits:
                            e.wait_ge(s, v)
                        if emit is not None:
                            ins = emit(e)
                            if inc is not None:
                                ins.then_inc(inc[0], inc[1])
                return body
            block.tensor(mk("pe"))
            block.scalar(mk("act"))
            block.vector(mk("dve"))
            block.gpsimd(mk("pool"))
            block.sync(mk("sp"))


def build(NSEQ=2, S=2048, debug=False, stop_phase=99):
    NT = S // 512
    assert NT >= 2
    NG = S // 128
    nc = bass.Bass("TRN2", target_bir_lowering=False)
    dt_in = lambda name, shape, dt=F32: nc.dram_tensor(name, shape, dt, kind="ExternalInput").ap()
    x_d = dt_in("x", [NSEQ * S, D])
    awin = dt_in("awin", [D, 4 * D])
    awout = dt_in("awout", [D, D])
    tab_d = dt_in("tab", [H, 128, 640])
    lwin = dt_in("lwin", [D, 2 * D])
    lwout = dt_in("lwout", [D, D])
    wa_d = dt_in("wa", [16, 128, 128])
    wx_d = dt_in("wx", [16, 128, 128])
    vec_d = dt_in("vecs", [128, 8, 16])
    lng_d = dt_in("lng", [2, D])
    lnb_d = dt_in("lnb", [2, D])
    ident_d = dt_in("ident", [128, 128])
    out_d = nc.dram_tensor("out", [NSEQ * S, D], F32, kind="ExternalOutput").ap()
    skind = "ExternalOutput" if debug else "Internal"
    scr = lambda name, shape, dt: nc.dram_tensor(name, shape, dt, kind=skind).ap()
    QT_d = scr("QT_s", [H, 128, S], BF16)
    KT_d = scr("KT_s", [H, 128, S], BF16)
    V_d = scr("V_s", [S, D], BF16)
    G_d = scr("G_s", [S, D], BF16)
    OGT_d = scr("OGT_s", [D, S], BF16)
    H1_d = scr("H1_s", [S, D], F32)
    YT_d = scr("YT_s", [D, S], BF16)
    WB_d = [nc.dram_tensor("WB%d_s" % i, [D, D], BF16).ap() for i in range(2)]

    with ExitStack() as stack:
        sb = lambda name, shape, dt: stack.enter_context(nc.sbuf_tensor(name, shape, dt))
        actT = sb("actT", [128, 16, S], BF16)
        wbuf = [sb("wbuf%d" % i, [128, 16, 512], BF16) for i in range(2)]
        identf = sb("identf", [128, 128], F32)
        identb = sb("identb", [128, 128], BF16)
        vecs = sb("vecs_sb", [128, 8, 16], F32)
        cl = sb("cl_sb", [128, 4, 16], F32)
        hb = sb("hb_sb", [128, 2, 16], F32)
        wa_sb = sb("wa_sb", [128, 16, 128], BF16)
        wx_sb = sb("wx_sb", [128, 16, 128], BF16)
        PADW = 25088
        pad = sb("pad", [128, PADW], F32)
        pb = [stack.enter_context(nc.psum_tensor("pb%d" % i, [128, 512], F32)) for i in range(8)]
        sc = Sched(nc, stack)

        class PadAlloc:
            def __init__(self):
                self.off = 0

            def f32(self, n):
                a = pad[:, self.off:self.off + n]
                self.off += n
                assert self.off <= PADW, self.off
                return a

            def bf16(self, n):
                assert n % 2 == 0
                a = pad[:, self.off:self.off + n // 2].bitcast(BF16)
                self.off += n // 2
                assert self.off <= PADW, self.off
                return a

        wslot = [0]

        def load_piece(w_ap, col0, ncols=512, dst_col=0):
            s = wslot[0]
            wv = w_ap.rearrange("(kc p) n -> p kc n", p=128)
            toks = []
            for hf in range(2):
                src = wv[:, hf * 8:(hf + 1) * 8, col0:col0 + ncols]
                dst = wbuf[s][:, hf * 8:(hf + 1) * 8, dst_col:dst_col + ncols]
                tk = "wbuf%d_h%d_c%d" % (s, hf, dst_col)
                sc.dma("pool", lambda e, dst=dst, src=src: e.dma_start(out=dst, in_=src),
                       reads=(), writes=(tk,))
                toks.append(tk)
            return tuple(toks)

        def load_piece_bf16(wb_ap, col0):
            s = wslot[0]
            wv = wb_ap.rearrange("(kc p) n -> p kc n", p=128)
            toks = []
            for hf in range(2):
                src = wv[:, hf * 8:(hf + 1) * 8, col0:col0 + 512]
                dst = wbuf[s][:, hf * 8:(hf + 1) * 8, :]
                tk = "wbuf%d_h%d_c0" % (s, hf)
                sc.dma("act", lambda e, dst=dst, src=src: e.dma_start(out=dst, in_=src), writes=(tk,))
                toks.append(tk)
            return tuple(toks)

        def next_slot():
            wslot[0] ^= 1

        sc.dma("sp", lambda e: e.dma_start(out=identf[:], in_=ident_d), writes=("identf",))
        sc.dma("pool", lambda e: e.dma_start(out=identb[:], in_=ident_d), writes=("identb",))
        sc.dma("sp", lambda e: e.dma_start(out=vecs[:], in_=vec_d), writes=("vecs",))
        sc.dma("pool", lambda e: e.dma_start(out=wa_sb[:], in_=wa_d.rearrange("n i j -> i n j")),
               writes=("wa",))
        sc.dma("pool", lambda e: e.dma_start(out=wx_sb[:], in_=wx_d.rearrange("n i j -> i n j")),
               writes=("wx",))
        sc.op("act", lambda e: e.activation(out=cl[:, 2, :], in_=vecs[:, 7, :], func=AF.Exp, scale=-1.0),
              reads=("vecs",), writes=("cl2",))
        sc.op("act", lambda e: e.activation(out=cl[:, 3, :], in_=cl[:, 2, :], func=AF.Ln, bias=1.0),
              reads=("cl2",), writes=("cl3",))
        sc.op("dve", lambda e: e.tensor_scalar(cl[:, 0, :], cl[:, 3, :], -8.0, None, ALU.mult),
              reads=("cl3",), writes=("cl0",))
        sc.op("dve", lambda e: e.tensor_scalar(cl[:, 1, :], cl[:, 3, :], -4.0, None, ALU.mult),
              reads=("cl3",), writes=("cl1",))
        sc.op("dve", lambda e: e.tensor_scalar(hb[:, :, :], vecs[:, 5:7, :], 0.5, None, ALU.mult),
              reads=("vecs",), writes=("hb",))
        sc.barrier()

        def transposes_f32(src, src_tok, g, bank0, evac_flip, all_act=False, tok=False):
            for b in range(4):
                bank = bank0 + (b % 2)
                for kk in range(4):
                    o = pb[bank][:, kk * 128:(kk + 1) * 128]
                    i_ = src[:, (4 * b + kk) * 128:(4 * b + kk + 1) * 128]
                    sc.op("pe", lambda e, o=o, i_=i_: e.transpose(o, i_, identf[:]),
                          reads=(src_tok, "identf"), writes=("pb%d" % bank,), signal=(kk == 3))
                dst = actT[:, 4 * b:4 * b + 4, g * 128:(g + 1) * 128]
                srcp = pb[bank][:, :].rearrange("p (k t) -> p k t", k=4)
                if all_act or (b + evac_flip) % 2 == 0:
                    sc.op("act", lambda e, dst=dst, srcp=srcp: e.activation(out=dst, in_=srcp, func=AF.Copy),
                          reads=("pb%d" % bank,), writes=(("aT%d_%d" % (g, b),) if tok else ()))
                else:
                    sc.op("dve", lambda e, dst=dst, srcp=srcp: e.tensor_copy(dst, srcp),
                          reads=("pb%d" % bank,), writes=(("aT%d_%d" % (g, b),) if tok else ()))

        def transposes_b16(src, src_tok, g, bank0, evac_flip):
            for b in range(4):
                bank = bank0 + (b % 2)
                pv = pb[bank][:, 0:256].bitcast(BF16)
                for kk in range(4):
                    o = pv[:, kk * 128:(kk + 1) * 128]
                    i_ = src[:, (4 * b + kk) * 128:(4 * b + kk + 1) * 128]
                    sc.op("pe", lambda e, o=o, i_=i_: e.transpose(o, i_, identb[:]),
                          reads=(src_tok, "identb"), writes=("pb%d" % bank,), signal=(kk == 3))
                dst = actT[:, 4 * b:4 * b + 4, g * 128:(g + 1) * 128]
                srcp = pv.rearrange("p (k t) -> p k t", k=4)
                if (b + evac_flip) % 2 == 0:
                    sc.op("act", lambda e, dst=dst, srcp=srcp: e.activation(out=dst, in_=srcp, func=AF.Copy),
                          reads=("pb%d" % bank,), writes=())
                else:
                    sc.op("dve", lambda e, dst=dst, srcp=srcp: e.tensor_copy(dst, srcp),
                          reads=("pb%d" % bank,), writes=())

        for sq in range(NSEQ):
            row0 = sq * S
            pa = PadAlloc()
            xs = [pa.f32(2048) for _ in range(4)]
            stg = [pa.bf16(4 * 512) for _ in range(2)]
            for g in range(NG):
                xb = xs[g % 4]
                src = x_d[row0 + g * 128: row0 + (g + 1) * 128, :]
                sc.dma("sp", lambda e, xb=xb, src=src: e.dma_start(out=xb, in_=src),
                       writes=("xs%d" % (g % 4),))
                transposes_f32(xb, "xs%d" % (g % 4), g, 2 * (g % 4), g, tok=True)
            if stop_phase <= 0:
                break
            nstage = [0]
            for p in range(16):
                s = wslot[0]
                wtoks = load_piece(awin, p * 512)
                if sq == 0 and p < 8:
                    wsrc = (awout if p < 4 else lwout)[:, (p % 4) * 512:(p % 4 + 1) * 512]
                    wdst = WB_d[p // 4][:, (p % 4) * 512:(p % 4 + 1) * 512]
                    sc.dma_bg("pool", lambda e, wdst=wdst, wsrc=wsrc: e.dma_start(out=wdst, in_=wsrc))
                kind_p = p // 4
                hp = p % 4
                for t in range(NT):
                    si = nstage[0] % 2
                    nstage[0] += 1
                    stv = stg[si].rearrange("p (j n) -> p j n", j=4)
                    stok = "stg%d" % si
                    for j in range(4):
                        bank = (4 * (t % 2) + j)
                        btok = "pb%d" % bank
                        for kc in range(16):
                            if kind_p < 2:
                                lhsT = wbuf[s][:, kc, j * 128:(j + 1) * 128]
                                rhs = actT[:, kc, t * 512:(t + 1) * 512]
                                atoks = tuple("aT%d_%d" % (4 * t + gg_, kc // 4) for gg_ in range(4))
                            else:
                                lhsT = actT[:, kc, t * 512 + j * 128: t * 512 + (j + 1) * 128]
                                rhs = wbuf[s][:, kc, :]
                                atoks = ("aT%d_%d" % (4 * t + j, kc // 4),)
                            sc.op("pe", lambda e, o=pb[bank][:, :], l=lhsT, r=rhs, kc=kc:
                                  e.matmul(o, l, r, start=(kc == 0), stop=(kc == 15)),
                                  reads=wtoks + (atoks if p == 0 else ()), writes=(btok,), signal=(kc == 15))
                        o = stv[:, j, :]
                        if kind_p == 0:
                            sc.op("act", lambda e, o=o, i_=pb[bank][:, :]: e.activation(
                                out=o, in_=i_, func=AF.Copy, scale=QSCALE),
                                reads=(btok,), writes=(stok,))
                        elif kind_p == 3:
                            sc.op("act", lambda e, o=o, i_=pb[bank][:, :]: e.activation(
                                out=o, in_=i_, func=AF.Silu), reads=(btok,), writes=(stok,))
                        else:
                            sc.op("dve", lambda e, o=o, i_=pb[bank][:, :]: e.tensor_copy(o, i_),
                                  reads=(btok,), writes=(stok,))
                    if kind_p < 2:
                        dd = (QT_d if kind_p == 0 else KT_d)[4 * hp:4 * hp + 4, :, t * 512:(t + 1) * 512]
                        dd = dd.rearrange("j p n -> p j n")
                    else:
                        dd = (V_d if kind_p == 2 else G_d)[t * 512:(t + 1) * 512, hp * 512:(hp + 1) * 512]
                        dd = dd.rearrange("(j p) n -> p j n", p=128)
                    sc.dma("sp", lambda e, dd=dd, stv=stv: e.dma_start(out=dd, in_=stv),
                           reads=(stok,), writes=())
                next_slot()
            sc.barrier()
            if stop_phase <= 1:
                continue

            pa = PadAlloc()
            ktb = [pa.bf16(S) for _ in range(2)]
            qtb = [pa.bf16(S) for _ in range(2)]
            vb = [pa.bf16(NG * 132).rearrange("p (t d) -> p t d", d=132) for _ in range(2)]
            gb = [pa.bf16(NG * 128).rearrange("p (t d) -> p t d", d=128) for _ in range(2)]
            tabb = [pa.bf16(640) for _ in range(2)]
            NSS = 5
            ssb = [pa.f32(512) for _ in range(NSS)]
            pTb = [pa.bf16(2560) for _ in range(2)]
            ogtok = [pa.bf16(128) for _ in range(8)]
            rcb = [pa.f32(2) for _ in range(8)]
            ogTs = [pa.bf16(S) for _ in range(2)]
            STB = (0, 1, 2, 5, 6)
            import os
            PREF = int(os.environ.get("PREF", "1"))
            NSTB = int(os.environ.get("NSTB", "5"))
            for i in range(2):
                sc.op("pool", lambda e, a=vb[i][:, :, 128:129]: e.memset(a, 1.0), writes=("v%d" % i,))

            def head_loads(h):
                hb = h % 2
                sc.dma("sp", lambda e, o=ktb[hb], i_=KT_d[h]: e.dma_start(out=o, in_=i_), writes=("kt%d" % hb,))
                sc.dma("sp", lambda e, o=qtb[hb], i_=QT_d[h]: e.dma_start(out=o, in_=i_), writes=("qt%d" % hb,))
                sc.dma("sp", lambda e, o=vb[hb][:, :, 0:128],
                       i_=V_d[:, h * 128:(h + 1) * 128].rearrange("(t p) d -> p t d", p=128):
                       e.dma_start(out=o, in_=i_), writes=("v%d" % hb,))
                sc.dma("sp", lambda e, o=gb[hb],
                       i_=G_d[:, h * 128:(h + 1) * 128].rearrange("(t p) d -> p t d", p=128):
                       e.dma_start(out=o, in_=i_), writes=("g%d" % hb,))
                sc.dma("pool", lambda e, o=tabb[hb], i_=tab_d[h]: e.dma_start(out=o, in_=i_), writes=("tab%d" % hb,))
                sc.op("pool", lambda e, a=tabb[hb][0:64, 576:640]: e.memset(a, NEG), writes=("tab%d" % hb,))
                sc.op("pool", lambda e, a=tabb[hb][64:128, 0:64]: e.memset(a, NEG), writes=("tab%d" % hb,))

            items2 = [(h, qi) for h in range(H) for qi in range(NT)]
            infos = {}
            nss = [0]

            def p2_f1(n):
                h, qi = items2[n]
                hb = h % 2
                if PREF and qi == 1 and h + 1 < H:
                    head_loads(h + 1)
                if (not PREF) and qi == 0 and h > 0:
                    head_loads(h)
                q0 = qi * 512
                pi = n % 2
                ptok = "pT%d" % pi
                info = {}
                off = 0
                for j in range(8):
                    if j in (2, 4, 6):
                        yield
                    J = 4 * qi - 4 + j
                    if J < 0:
                        continue
                    qlo = max(0, 2 * j - 8)
                    qhi = min(7, 2 * j + 1)
                    w = 64 * (qhi - qlo + 1)
                    c0 = 64 * (qlo - 2 * j + 8)
                    info[j] = (J, qlo, w, off)
                    k_ = nss[0]
                    nss[0] += 1
                    bank = STB[k_ % NSTB]
                    si = k_ % NSS
                    sc.op("pe", lambda e, o=pb[bank][:, 0:w], l=ktb[hb][:, J * 128:(J + 1) * 128],
                          r=qtb[hb][:, q0 + 64 * qlo: q0 + 64 * qlo + w]:
                          e.matmul(o, l, r, start=True, stop=False),
                          reads=("kt%d" % hb, "qt%d" % hb), writes=("pb%d" % bank,), signal=False)
                    sc.op("pe", lambda e, o=pb[bank][:, 0:w], l=identb[:], r=tabb[hb][:, c0:c0 + w]:
                          e.matmul(o, l, r, start=False, stop=True),
                          reads=("identb", "tab%d" % hb), writes=("pb%d" % bank,))
                    sc.op("act", lambda e, o=pTb[pi][:, off:off + w], a=pb[bank][:, 0:w]:
                          e.activation(out=o, in_=a, func=AF.Exp),
                          reads=("pb%d" % bank,), writes=(ptok,))
                    off += w
                infos[n] = info
                yield

            def p2_f2(n):
                h, qi = items2[n]
                hb = h % 2
                pi = n % 2
                ptok = "pT%d" % pi
                info = infos.pop(n)
                for g in range(4):
                    if g > 0:
                        yield
                    js = [j for j in range(g, g + 5) if j in info]
                    obank = 3 + (g % 2)
                    ocol = 0
                    otok = "pb%d" % obank
                    oap = pb[obank][:, ocol:ocol + 129]
                    for idx, j in enumerate(js):
                        J, qlo, w, poff = info[j]
                        c = poff + 64 * (2 * g - qlo)
                        sc.op("pe", lambda e, o=oap, l=pTb[pi][:, c:c + 128], r=vb[hb][:, J, 0:129],
                              st=(idx == 0), sp_=(idx == len(js) - 1):
                              e.matmul(o, l, r, start=st, stop=sp_),
                              reads=(ptok, "v%d" % hb), writes=(otok,), signal=(idx == len(js) - 1))
                    oi = (n % 2) * 4 + g
                    sc.op("dve", lambda e, o=rcb[oi][:, 0:1], a=pb[obank][:, ocol + 128:ocol + 129]:
                          e.reciprocal(o, a), reads=(otok,), writes=("rc%d" % oi,))
                    sc.op("dve", lambda e, o=ogtok[oi], a=pb[obank][:, ocol:ocol + 128], s_=rcb[oi][:, 0:1],
                          b=gb[hb][:, 4 * qi + g, :]:
                          e.scalar_tensor_tensor(o, a, s_, b, ALU.mult, ALU.mult),
                          reads=(otok, "rc%d" % oi, "g%d" % hb), writes=("ogtok%d" % oi,))

            def p2_f3(n):
                h, qi = items2[n]
                hb = h % 2
                q0 = qi * 512
                for g in range(4):
                    oi = (n % 2) * 4 + g
                    tpo = pb[7][:, 0:256].bitcast(BF16)[:, g * 128:(g + 1) * 128]
                    sc.op("pe", lambda e, o=tpo, a=ogtok[oi]: e.transpose(o, a, identb[:]),
                          reads=("ogtok%d" % oi, "identb"), writes=("pb7",), signal=(g == 3))
                sc.op("act", lambda e, o=ogTs[hb][:, q0:q0 + 512], a=pb[7][:, 0:256].bitcast(BF16):
                      e.activation(out=o, in_=a, func=AF.Copy),
                      reads=("pb7",), writes=("ogTs%d" % hb,))
                if qi == NT - 1:
                    sc.dma("sp", lambda e, o=OGT_d[h * 128:(h + 1) * 128, :], a=ogTs[hb]:
                           e.dma_start(out=o, in_=a), reads=("ogTs%d" % hb,))

            head_loads(0)
            NI = len(items2)
            import os
            SK1 = int(os.environ.get("SK1", "1"))
            SK2 = int(os.environ.get("SK2", "2"))
            for step in range(NI + SK2):
                g1 = p2_f1(step) if step < NI else iter(())
                g2 = p2_f2(step - SK1) if 0 <= step - SK1 < NI else iter(())
                for _ in range(5):
                    next(g1, None)
                    next(g2, None)
                for _ in g1:
                    pass
                for _ in g2:
                    pass
                if 0 <= step - SK2 < NI:
                    p2_f3(step - SK2)
            sc.barrier(bg=True)
            if stop_phase <= 2:
                continue

            def outproj_ln(srcT_d, w_d, layer, resid_d, resid_row0, dst_d, dst_row0, do_T):
                pa = PadAlloc()
                ogt = pa.bf16(16 * 512).rearrange("p (k n) -> p k n", k=16)
                xz = [[pa.f32(2048) for _ in range(4)] for _ in range(2)]
                gbc = pa.f32(2048)
                bbc = pa.f32(2048)
                stats = [pa.f32(24) for _ in range(2)]
                mv = [pa.f32(8) for _ in range(2)]
                nbk = [0]
                sc.dma("sp", lambda e: e.dma_start(out=gbc, in_=lng_d[layer, :].partition_broadcast(128)),
                       writes=("gbc",))
                sc.dma("sp", lambda e: e.dma_start(out=bbc, in_=lnb_d[layer, :].partition_broadcast(128)),
                       writes=("bbc",))
                sv = srcT_d.rearrange("(kc p) n -> p kc n", p=128)
                xtok = lambda t, i: "xz%d_%d" % (t % 2, i)

                ogtb = None if do_T else [actT[:, :, 0:512], actT[:, :, 512:1024]]

                def ogt_of(t):
                    return ogt if do_T else ogtb[t % 2]

                def ogt_toks(t, i):
                    return ("ogtq%d" % i,) if do_T else ("ogtb%dh0" % (t % 2), "ogtb%dh1" % (t % 2))

                def ogt_load(t):
                    if do_T:
                        for i in range(4):
                            sc.dma("sp", lambda e, o=ogt[:, :, i * 128:(i + 1) * 128],
                                   i_=sv[:, :, t * 512 + i * 128: t * 512 + (i + 1) * 128]:
                                   e.dma_start(out=o, in_=i_), writes=("ogtq%d" % i,))
                    else:
                        for hf in range(2):
                            sc.dma("sp", lambda e, o=ogtb[t % 2][:, hf * 8:(hf + 1) * 8, :],
                                   i_=sv[:, hf * 8:(hf + 1) * 8, t * 512:(t + 1) * 512]:
                                   e.dma_start(out=o, in_=i_), writes=("ogtb%dh%d" % (t % 2, hf),))

                def xz_loads(t):
                    for i in range(4):
                        r0 = resid_row0 + t * 512 + i * 128
                        sc.dma("sp", lambda e, o=xz[t % 2][i], i_=resid_d[r0:r0 + 128, :]:
                               e.dma_start(out=o, in_=i_), writes=(xtok(t, i),))

                wq = []

                def prefetch(n):
                    s_ = wslot[0]
                    wq.append((s_, load_piece_bf16(w_d, n * 512)))
                    next_slot()

                def mm_piece(t, n):
                    s, wtoks = wq.pop(0)
                    for i in range(4):
                        bank = nbk[0] % 6
                        nbk[0] += 1
                        btok = "pb%d" % bank
                        for kc in range(16):
                            sc.op("pe", lambda e, o=pb[bank][:, :], l=ogt_of(t)[:, kc, i * 128:(i + 1) * 128],
                                  r=wbuf[s][:, kc, :], kc=kc:
                                  e.matmul(o, l, r, start=(kc == 0), stop=(kc == 15)),
                                  reads=wtoks + ogt_toks(t, i), writes=(btok,), signal=(kc == 15))
                        zz = xz[t % 2][i][:, n * 512:(n + 1) * 512]
                        sc.op("dve", lambda e, zz=zz, a=pb[bank][:, :]:
                              e.scalar_tensor_tensor(zz, zz, ALPHA, a, ALU.mult, ALU.add),
                              reads=(btok, xtok(t, i)), writes=(xtok(t, i),))
                    if 4 * t + n + 2 < 4 * NT:
                        prefetch((n + 2) % 4)

                def ln_group(t, i):
                    si = i % 2
                    xt = xtok(t, i)
                    z = xz[t % 2][i]
                    for c4 in range(4):
                        sc.op("dve", lambda e, o=stats[si][:, c4 * 6:(c4 + 1) * 6],
                              a=z[:, c4 * 512:(c4 + 1) * 512]: e.bn_stats(o, a),
                              reads=(xt,), writes=("stats%d" % si,))
                    sc.op("dve", lambda e, o=mv[si][:, 0:2], a=stats[si]: e.bn_aggr(o, a),
                          reads=("stats%d" % si,), writes=("mv%d" % si,))
                    sc.op("dve", lambda e, o=mv[si][:, 2:3], a=mv[si][:, 1:2]:
                          e.tensor_scalar(o, a, LN_EPS, None, ALU.add),
                          reads=("mv%d" % si,), writes=("mvb%d" % si,))
                    sc.op("act", lambda e, o=mv[si][:, 3:4], a=mv[si][:, 2:3]:
                          e.activation(out=o, in_=a, func=AF.Sqrt),
                          reads=("mvb%d" % si,), writes=("mvc%d" % si,))
                    sc.op("dve", lambda e, o=mv[si][:, 4:5], a=mv[si][:, 3:4]: e.reciprocal(o, a),
                          reads=("mvc%d" % si,), writes=("mvd%d" % si,))
                    sc.op("dve", lambda e, z=z, m=mv[si][:, 0:1]:
                          e.scalar_tensor_tensor(z, z, m, gbc, ALU.subtract, ALU.mult),
                          reads=(xt, "mv%d" % si, "gbc"), writes=(xt,))
                    sc.op("act", lambda e, z=z, r=mv[si][:, 4:5]:
                          e.activation(out=z, in_=z, func=AF.Identity, scale=r),
                          reads=(xt, "mvd%d" % si), writes=(xt,))
                    sc.op("pool", lambda e, z=z: e.tensor_tensor(z, z, bbc, ALU.add),
                          reads=(xt, "bbc"), writes=(xt,))
                    r0 = dst_row0 + t * 512 + i * 128
                    sc.dma("sp", lambda e, o=dst_d[r0:r0 + 128, :], a=z: e.dma_start(out=o, in_=a),
                           reads=(xt,))

                def tr_group(t, i):
                    transposes_f32(xz[t % 2][i], xtok(t, i), 4 * t + i, 6, 0, all_act=True)

                ogt_load(0)
                xz_loads(0)
                prefetch(0)
                prefetch(1)
                for t in range(NT):
                    mm_piece(t, 0)
                    if (not do_T) and t + 1 < NT:
                        ogt_load(t + 1)
                    if t > 0:
                        ln_group(t - 1, 0)
                        ln_group(t - 1, 1)
                    mm_piece(t, 1)
                    if t > 0:
                        ln_group(t - 1, 2)
                        ln_group(t - 1, 3)
                    mm_piece(t, 2)
                    if t > 0 and do_T:
                        for i in range(4):
                            tr_group(t - 1, i)
                    if t + 1 < NT:
                        xz_loads(t + 1)
                    mm_piece(t, 3)
                    if do_T and t + 1 < NT:
                        ogt_load(t + 1)
                for i in range(4):
                    ln_group(NT - 1, i)
                    if do_T:
                        tr_group(NT - 1, i)
                sc.barrier()

            outproj_ln(OGT_d, WB_d[0], 0, x_d, row0, H1_d, 0, True)
            if stop_phase <= 3:
                continue

            pa = PadAlloc()
            upad = [pa.f32(S + 4) for _ in range(2)]
            ysb = [pa.bf16(S) for _ in range(2)]
            NTB = 3
            uc = [pa.f32(512) for _ in range(NTB)]
            ucb = [pa.bf16(512) for _ in range(NTB)]
            rr = [pa.f32(512) for _ in range(NTB)]
            ii = [pa.f32(512) for _ in range(NTB)]
            aa = [pa.f32(512) for _ in range(NTB)]
            mm = [pa.f32(512) for _ in range(NTB)]
            bb = [pa.f32(512) for _ in range(NTB)]
            hh = [pa.f32(512) for _ in range(NTB)]
            sg = [pa.f32(512) for _ in range(NTB)]
            gg = [pa.f32(512) for _ in range(NTB)]
            for i in range(2):
                sc.op("pool", lambda e, a=upad[i][:, 0:3]: e.memset(a, 0.0), writes=("upad%d" % i,))
            wt4 = {}

            def p4_wload(c2):
                wt4[c2] = (wslot[0],
                           load_piece(lwin, c2 * 256, 256, 0) + load_piece(lwin, D + c2 * 256, 256, 256))
                next_slot()

            NI4 = 16 * NT

            def p4_ids(n):
                c, t = n // NT, n % NT
                return c, t, c % 2, n % NTB, "_%d" % (n % NTB)

            def p4_f1a(n):
                c, t, ub, tb, T = p4_ids(n)
                c2, jj = c // 2, c % 2
                if jj == 0 and t == 0 and c2 + 1 < 8:
                    p4_wload(c2 + 1)
                s, wtoks = wt4[c2]
                bu, bg = 2 * (n % 2), 2 * (n % 2) + 1
                for (bank, colb) in ((bu, jj * 128), (bg, 256 + jj * 128)):
                    for kc in range(16):
                        sc.op("pe", lambda e, o=pb[bank][:, :], l=wbuf[s][:, kc, colb:colb + 128],
                              r=actT[:, kc, t * 512:(t + 1) * 512], kc=kc:
                              e.matmul(o, l, r, start=(kc == 0), stop=(kc == 15)),
                              reads=wtoks, writes=("pb%d" % bank,), signal=(kc == 15))

            def p4_f1b(n):
                c, t, ub, tb, T = p4_ids(n)
                bu, bg = 2 * (n % 2), 2 * (n % 2) + 1
                utok = "upad%d" % ub
                sc.op("act", lambda e, o=upad[ub][:, 3 + t * 512: 3 + (t + 1) * 512], a=pb[bu][:, :]:
                      e.activation(out=o, in_=a, func=AF.Copy),
                      reads=("pb%d" % bu,), writes=(utok + "_t%d" % t,))
                sc.op("act", lambda e, o=sg[tb], a=pb[bg][:, :]: e.activation(out=o, in_=a, func=AF.Tanh, scale=0.5),
                      reads=("pb%d" % bg,), writes=("sg" + T,))
                sc.op("dve", lambda e, o=gg[tb], a=pb[bg][:, :]: e.tensor_copy(o, a),
                      reads=("pb%d" % bg, "sg" + T), writes=("gg" + T,))

            def p4_f2(n):
                c, t, ub, tb, T = p4_ids(n)
                utok = "upad%d" % ub
                br, bi = 4 + 2 * (n % 2), 5 + 2 * (n % 2)
                ureads = (utok, utok + "_t%d" % t) + ((utok + "_t%d" % (t - 1),) if t > 0 else ())
                sc.op("dve", lambda e, o=uc[tb], a=upad[ub][:, t * 512: t * 512 + 512],
                      w0=vecs[:, 0, c:c + 1], cb=vecs[:, 4, c:c + 1]:
                      e.tensor_scalar(o, a, w0, cb, ALU.mult, ALU.add),
                      reads=ureads + ("vecs",), writes=("uc" + T,))
                for k in range(1, 4):
                    sc.op("dve", lambda e, o=uc[tb], a=upad[ub][:, t * 512 + k: t * 512 + k + 512],
                          wk=vecs[:, k, c:c + 1]:
                          e.scalar_tensor_tensor(o, a, wk, o, ALU.mult, ALU.add),
                          reads=ureads + ("vecs", "uc" + T), writes=("uc" + T,))
                sc.op("act", lambda e, o=ucb[tb], a=uc[tb]: e.activation(out=o, in_=a, func=AF.Copy),
                      reads=("uc" + T,), writes=("ucb" + T,))
                sc.op("pe", lambda e, o=pb[br][:, :], l=wa_sb[:, c, :], r=ucb[tb]:
                      e.matmul(o, l, r, start=True, stop=True),
                      reads=("wa", "ucb" + T), writes=("pb%d" % br,))
                sc.op("pe", lambda e, o=pb[bi][:, :], l=wx_sb[:, c, :], r=ucb[tb]:
                      e.matmul(o, l, r, start=True, stop=True),
                      reads=("wx", "ucb" + T), writes=("pb%d" % bi,))

            def p4_f3a(n):
                c, t, ub, tb, T = p4_ids(n)
                br, bi = 4 + 2 * (n % 2), 5 + 2 * (n % 2)
                sc.op("act", lambda e, o=rr[tb], a=pb[br][:, :], b=hb[:, 0, c:c + 1]:
                      e.activation(out=o, in_=a, func=AF.Tanh, bias=b, scale=0.5),
                      reads=("pb%d" % br, "hb"), writes=("rr" + T,))
                sc.op("act", lambda e, o=ii[tb], a=pb[bi][:, :], b=hb[:, 1, c:c + 1]:
                      e.activation(out=o, in_=a, func=AF.Tanh, bias=b, scale=0.5),
                      reads=("pb%d" % bi, "hb"), writes=("ii" + T,))
                sc.op("act", lambda e, o=aa[tb], a=rr[tb], s_=cl[:, 1, c:c + 1]:
                      e.activation(out=o, in_=a, func=AF.Exp, scale=s_, bias=s_),
                      reads=("rr" + T, "cl1"), writes=("aa" + T,))
                sc.op("act", lambda e, o=mm[tb], a=rr[tb], s_=cl[:, 0, c:c + 1]:
                      e.activation(out=o, in_=a, func=AF.Exp, scale=s_, bias=s_),
                      reads=("rr" + T, "cl0"), writes=("mm" + T,))
                sc.op("dve", lambda e, o=bb[tb], a=ii[tb], b=uc[tb]:
                      e.scalar_tensor_tensor(o, a, 1.0, b, ALU.add, ALU.mult),
                      reads=("ii" + T, "uc" + T), writes=("bb" + T,))
                sc.op("act", lambda e, o=mm[tb]: e.activation(out=o, in_=o, func=AF.Abs, scale=-0.25, bias=0.25),
                      reads=("mm" + T,), writes=("mm" + T,))
                sc.op("act", lambda e, o=mm[tb]: e.activation(out=o, in_=o, func=AF.Sqrt),
                      reads=("mm" + T,), writes=("mm" + T,))
                sc.op("pool", lambda e, o=bb[tb], a=mm[tb]: e.tensor_tensor(o, o, a, ALU.mult),
                      reads=("bb" + T, "mm" + T), writes=("bb" + T,))
                sc.op("pool", lambda e, o=sg[tb]: e.tensor_scalar(o, o, 0.5, 0.5, ALU.mult, ALU.add),
                      reads=("sg" + T,), writes=("sg" + T,))
                sc.op("pool", lambda e, o=sg[tb], a=gg[tb]: e.tensor_tensor(o, o, a, ALU.mult),
                      reads=("sg" + T, "gg" + T), writes=("sg" + T,))

            def p4_f3b(n):
                c, t, ub, tb, T = p4_ids(n)
                ptb = (n - 1) % NTB
                init = 0.0 if t == 0 else hh[ptb][:, 511:512]
                sc.op("dve", lambda e, o=hh[tb], a=aa[tb], b=bb[tb], init=init:
                      e.tensor_tensor_scan(o, a, b, init, ALU.mult, ALU.add),
                      reads=("aa" + T, "bb" + T) + (("hh_%d" % ptb,) if t > 0 else ()),
                      writes=("hh" + T,))
                sc.op("pool", lambda e, o=ysb[ub][:, t * 512:(t + 1) * 512], a=hh[tb], b=sg[tb]:
                      e.tensor_tensor(o, a, b, ALU.mult),
                      reads=("hh" + T, "sg" + T), writes=("ysb%d" % ub,))
                if t == NT - 1:
                    sc.dma("sp", lambda e, o=YT_d[c * 128:(c + 1) * 128, :], a=ysb[ub]: e.dma_start(out=o, in_=a),
                           reads=("ysb%d" % ub,))

            p4_wload(0)
            for step in range(NI4 + 2):
                if step < NI4:
                    p4_f1a(step)
                if 0 <= step - 2 < NI4:
                    p4_f3a(step - 2)
                if 0 <= step - 1 < NI4:
                    p4_f2(step - 1)
                if 0 <= step - 2 < NI4:
                    p4_f3b(step - 2)
                if step < NI4:
                    p4_f1b(step)
            sc.barrier()
            if stop_phase <= 4:
                continue
            outproj_ln(YT_d, WB_d[1], 1, H1_d, 0, out_d, row0, False)

        sc.barrier()
        sc.emit_all()
    return nc


def _prep_shared(inp):
    f = lambda a: np.ascontiguousarray(np.asarray(a, dtype=np.float32))
    k = np.arange(128)[:, None]
    xq = np.arange(640)[None, :]
    idx = np.minimum(xq - k, 256) + 256
    tab = f(np.asarray(inp["attn_rel_bias"])[0][:, idx])
    cw = np.asarray(inp["lru_conv_w"])[0]
    vec = np.stack([cw[0], cw[1], cw[2], cw[3], np.asarray(inp["lru_conv_b"])[0],
                    np.asarray(inp["lru_ba"])[0].reshape(-1), np.asarray(inp["lru_bx"])[0].reshape(-1),
                    np.asarray(inp["lru_lambda"])[0]], axis=0)
    vecs = f(vec.reshape(8, 16, 128).transpose(2, 0, 1))
    return {
        "awin": f(np.asarray(inp["attn_w_in"])[0]), "awout": f(np.asarray(inp["attn_w_out"])[0]),
        "tab": tab, "lwin": f(np.asarray(inp["lru_w_in"])[0]), "lwout": f(np.asarray(inp["lru_w_out"])[0]),
        "wa": f(np.asarray(inp["lru_wa"])[0]), "wx": f(np.asarray(inp["lru_wx"])[0]), "vecs": vecs,
        "lng": f(inp["ln_gain"]), "lnb": f(inp["ln_bias"]), "ident": np.eye(128, dtype=np.float32),
    }


def kernel(**inputs):
    x = np.asarray(inputs["x"], dtype=np.float32)
    B, S, _ = x.shape
    nseq = B // N_CORES
    shared = _prep_shared(inputs)
    nc = build(NSEQ=nseq, S=S)
    in_maps = []
    for c in range(N_CORES):
        m = dict(shared)
        m["x"] = np.ascontiguousarray(x[c * nseq:(c + 1) * nseq].reshape(nseq * S, D))
        in_maps.append(m)
    res = run_bass_kernel_spmd(nc, in_maps, core_ids=list(range(N_CORES)))
    out = np.concatenate([np.asarray(r["out"]).reshape(nseq, S, D) for r in res.results], axis=0)
    return out.astype(np.float32)
```

```python
import numpy as np
from contextlib import ExitStack
import concourse.bass as bass
import concourse.mybir as mybir
from concourse.bass_utils import run_bass_kernel_spmd

F32 = mybir.dt.float32
BF16 = mybir.dt.bfloat16
AF = mybir.ActivationFunctionType
ALU = mybir.AluOpType

D = 2048
H = 16
ALPHA = (2.0 * 2) ** 0.25
LN_EPS = 1e-5
NEG = -30000.0
QSCALE = 128 ** -0.5
N_CORES = 8


class Ev:
    __slots__ = ("sem", "val", "eng")

    def __init__(self, sem, val, eng):
        self.sem, self.val, self.eng = sem, val, eng


class Sched:
    ENGS = ("pe", "act", "dve", "pool", "sp")
    DMAQ = ("sp", "pool", "act")

    def __init__(self, nc, stack, ndma=8):
        self.nc = nc
        self.sem = {e: stack.enter_context(nc.semaphore("c_" + e)) for e in self.ENGS}
        self.cnt = dict.fromkeys(self.ENGS, 0)
        self.prog = {e: [] for e in self.ENGS}
        self.known = {e: {} for e in self.ENGS}
        self.dsem = {q: [stack.enter_context(nc.semaphore("d_%s%d" % (q, i))) for i in range(ndma)]
                     for q in self.DMAQ}
        self.dcnt = {q: [0] * ndma for q in self.DMAQ}
        self.drr = dict.fromkeys(self.DMAQ, 0)
        self.tok = {}
        self.defer = {e: ([], []) for e in self.ENGS}
        self.last = dict.fromkeys(self.ENGS, None)
        self.bgsem = [stack.enter_context(nc.semaphore("bg%d" % i)) for i in range(8)]
        self.bgcnt = [0] * 8
        self.bgrr = 0

    def dma_bg(self, q, emit):
        i = self.bgrr
        self.bgrr = (i + 1) % 8
        sem = self.bgsem[i]
        prev = self.bgcnt[i]
        waits = self._need(q, [Ev(sem, prev, "dma")]) if prev else []
        self.bgcnt[i] = prev + 16
        self.prog[q].append((waits, emit, (sem, 16)))

    def _need(self, eng, evs):
        kn = self.known[eng]
        best = {}
        for ev in evs:
            if ev is None:
                continue
            if ev.eng == eng and eng == "pe":
                continue
            k = ev.sem
            if kn.get(k, 0) >= ev.val:
                continue
            if best.get(k, 0) < ev.val:
                best[k] = ev.val
        waits = []
        for k, v in best.items():
            kn[k] = v
            waits.append((k, v))
        return waits

    def _deps(self, reads, writes):
        evs = []
        for t in reads:
            st = self.tok.get(t)
            if st is not None and st[0] is not None:
                evs.append(st[0])
        for t in writes:
            st = self.tok.get(t)
            if st is not None:
                if st[0] is not None:
                    evs.append(st[0])
                evs.extend(st[1].values())
        return evs

    def _commit(self, ev, reads, writes):
        key = ev.sem
        for t in reads:
            st = self.tok.get(t)
            if st is None:
                st = self.tok[t] = [None, {}]
            st[1][key] = ev
        for t in writes:
            self.tok[t] = [ev, {}]

    def op(self, eng, emit, reads=(), writes=(), signal=True):
        evs = self._deps(reads, writes)
        waits = self._need(eng, evs)
        dr, dw = self.defer[eng]
        if signal:
            self.cnt[eng] += 1
            ev = Ev(self.sem[eng], self.cnt[eng], eng)
            self.prog[eng].append((waits, emit, (self.sem[eng], 1)))
            self._commit(ev, list(reads) + dr, list(writes) + dw)
            self.defer[eng] = ([], [])
            self.last[eng] = ev
        else:
            self.prog[eng].append((waits, emit, None))
            dr.extend(reads)
            dw.extend(writes)

    def dma(self, q, emit, reads=(), writes=()):
        evs = self._deps(reads, writes)
        i = self.drr[q]
        self.drr[q] = (i + 1) % len(self.dsem[q])
        sem = self.dsem[q][i]
        prev = self.dcnt[q][i]
        if prev:
            evs.append(Ev(sem, prev, "dma"))
        waits = self._need(q, evs)
        self.dcnt[q][i] = prev + 16
        ev = Ev(sem, prev + 16, "dma")
        self.prog[q].append((waits, emit, (sem, 16)))
        self._commit(ev, reads, writes)

    def barrier(self, bg=False):
        for e in self.ENGS:
            assert not self.defer[e][0] and not self.defer[e][1], e
        evs = [self.last[e] for e in self.ENGS if self.last[e] is not None]
        if bg:
            for i, c in enumerate(self.bgcnt):
                if c:
                    evs.append(Ev(self.bgsem[i], c, "dma"))
        for q in self.DMAQ:
            for i, c in enumerate(self.dcnt[q]):
                if c:
                    evs.append(Ev(self.dsem[q][i], c, "dma"))
        for e in self.ENGS:
            w = self._need(e, evs)
            if w:
                self.prog[e].append((w, None, None))
        self.tok = {}

    def emit_all(self):
        nc = self.nc
        with nc.Block() as block:
            def mk(name):
                def body(e):
                    for waits, emit, inc in self.prog[name]:
                        for s, v in waits:
                            e.wait_ge(s, v)
                        if emit is not None:
                            ins = emit(e)
                            if inc is not None:
                                ins.then_inc(inc[0], inc[1])
                return body
            block.tensor(mk("pe"))
            block.scalar(mk("act"))
            block.vector(mk("dve"))
            block.gpsimd(mk("pool"))
            block.sync(mk("sp"))


def build(NSEQ=2, S=2048, debug=False, stop_phase=99):
    NT = S // 512
    assert NT >= 2
    NG = S // 128
    nc = bass.Bass("TRN2", target_bir_lowering=False)
    dt_in = lambda name, shape, dt=F32: nc.dram_tensor(name, shape, dt, kind="ExternalInput").ap()
    x_d = dt_in("x", [NSEQ * S, D])
    awin = dt_in("awin", [D, 4 * D])
    awout = dt_in("awout", [D, D])
    tab_d = dt_in("tab", [H, 128, 640])
    lwin = dt_in("lwin", [D, 2 * D])
    lwout = dt_in("lwout", [D, D])
    wa_d = dt_in("wa", [16, 128, 128])
    wx_d = dt_in("wx", [16, 128, 128])
    vec_d = dt_in("vecs", [128, 8, 16])
    lng_d = dt_in("lng", [2, D])
    lnb_d = dt_in("lnb", [2, D])
    ident_d = dt_in("ident", [128, 128])
    out_d = nc.dram_tensor("out", [NSEQ * S, D], F32, kind="ExternalOutput").ap()
    skind = "ExternalOutput" if debug else "Internal"
    scr = lambda name, shape, dt: nc.dram_tensor(name, shape, dt, kind=skind).ap()
    QT_d = scr("QT_s", [H, 128, S], BF16)
    KT_d = scr("KT_s", [H, 128, S], BF16)
    V_d = scr("V_s", [S, D], BF16)
    G_d = scr("G_s", [S, D], BF16)
    OGT_d = scr("OGT_s", [D, S], BF16)
    H1_d = scr("H1_s", [S, D], F32)
    YT_d = scr("YT_s", [D, S], BF16)
    WB_d = [nc.dram_tensor("WB%d_s" % i, [D, D], BF16).ap() for i in range(2)]

    with ExitStack() as stack:
        sb = lambda name, shape, dt: stack.enter_context(nc.sbuf_tensor(name, shape, dt))
        actT = sb("actT", [128, 16, S], BF16)
        wbuf = [sb("wbuf%d" % i, [128, 16, 512], BF16) for i in range(2)]
        identf = sb("identf", [128, 128], F32)
        identb = sb("identb", [128, 128], BF16)
        vecs = sb("vecs_sb", [128, 8, 16], F32)
        cl = sb("cl_sb", [128, 4, 16], F32)
        hb = sb("hb_sb", [128, 2, 16], F32)
        wa_sb = sb("wa_sb", [128, 16, 128], BF16)
        wx_sb = sb("wx_sb", [128, 16, 128], BF16)
        PADW = 25088
        pad = sb("pad", [128, PADW], F32)
        pb = [stack.enter_context(nc.psum_tensor("pb%d" % i, [128, 512], F32)) for i in range(8)]
        sc = Sched(nc, stack)

        class PadAlloc:
            def __init__(self):
                self.off = 0

            def f32(self, n):
                a = pad[:, self.off:self.off + n]
                self.off += n
                assert self.off <= PADW, self.off
                return a

            def bf16(self, n):
                assert n % 2 == 0
                a = pad[:, self.off:self.off + n // 2].bitcast(BF16)
                self.off += n // 2
                assert self.off <= PADW, self.off
                return a

        wslot = [0]

        def load_piece(w_ap, col0, ncols=512, dst_col=0):
            s = wslot[0]
            wv = w_ap.rearrange("(kc p) n -> p kc n", p=128)
            toks = []
            for hf in range(2):
                src = wv[:, hf * 8:(hf + 1) * 8, col0:col0 + ncols]
                dst = wbuf[s][:, hf * 8:(hf + 1) * 8, dst_col:dst_col + ncols]
                tk = "wbuf%d_h%d_c%d" % (s, hf, dst_col)
                sc.dma("pool", lambda e, dst=dst, src=src: e.dma_start(out=dst, in_=src),
                       reads=(), writes=(tk,))
                toks.append(tk)
            return tuple(toks)

        def load_piece_bf16(wb_ap, col0):
            s = wslot[0]
            wv = wb_ap.rearrange("(kc p) n -> p kc n", p=128)
            toks = []
            for hf in range(2):
                src = wv[:, hf * 8:(hf + 1) * 8, col0:col0 + 512]
                dst = wbuf[s][:, hf * 8:(hf + 1) * 8, :]
                tk = "wbuf%d_h%d_c0" % (s, hf)
                sc.dma("act", lambda e, dst=dst, src=src: e.dma_start(out=dst, in_=src), writes=(tk,))
                toks.append(tk)
            return tuple(toks)

        def next_slot():
            wslot[0] ^= 1

        sc.dma("sp", lambda e: e.dma_start(out=identf[:], in_=ident_d), writes=("identf",))
        sc.dma("pool", lambda e: e.dma_start(out=identb[:], in_=ident_d), writes=("identb",))
        sc.dma("sp", lambda e: e.dma_start(out=vecs[:], in_=vec_d), writes=("vecs",))
        sc.dma("pool", lambda e: e.dma_start(out=wa_sb[:], in_=wa_d.rearrange("n i j -> i n j")),
               writes=("wa",))
        sc.dma("pool", lambda e: e.dma_start(out=wx_sb[:], in_=wx_d.rearrange("n i j -> i n j")),
               writes=("wx",))
        sc.op("act", lambda e: e.activation(out=cl[:, 2, :], in_=vecs[:, 7, :], func=AF.Exp, scale=-1.0),
              reads=("vecs",), writes=("cl2",))
        sc.op("act", lambda e: e.activation(out=cl[:, 3, :], in_=cl[:, 2, :], func=AF.Ln, bias=1.0),
              reads=("cl2",), writes=("cl3",))
        sc.op("dve", lambda e: e.tensor_scalar(cl[:, 0, :], cl[:, 3, :], -8.0, None, ALU.mult),
              reads=("cl3",), writes=("cl0",))
        sc.op("dve", lambda e: e.tensor_scalar(cl[:, 1, :], cl[:, 3, :], -4.0, None, ALU.mult),
              reads=("cl3",), writes=("cl1",))
        sc.op("dve", lambda e: e.tensor_scalar(hb[:, :, :], vecs[:, 5:7, :], 0.5, None, ALU.mult),
              reads=("vecs",), writes=("hb",))
        sc.barrier()

        def transposes_f32(src, src_tok, g, bank0, evac_flip, all_act=False, tok=False):
            for b in range(4):
                bank = bank0 + (b % 2)
                for kk in range(4):
                    o = pb[bank][:, kk * 128:(kk + 1) * 128]
                    i_ = src[:, (4 * b + kk) * 128:(4 * b + kk + 1) * 128]
                    sc.op("pe", lambda e, o=o, i_=i_: e.transpose(o, i_, identf[:]),
                          reads=(src_tok, "identf"), writes=("pb%d" % bank,), signal=(kk == 3))
                dst = actT[:, 4 * b:4 * b + 4, g * 128:(g + 1) * 128]
                srcp = pb[bank][:, :].rearrange("p (k t) -> p k t", k=4)
                if all_act or (b + evac_flip) % 2 == 0:
                    sc.op("act", lambda e, dst=dst, srcp=srcp: e.activation(out=dst, in_=srcp, func=AF.Copy),
                          reads=("pb%d" % bank,), writes=(("aT%d_%d" % (g, b),) if tok else ()))
                else:
                    sc.op("dve", lambda e, dst=dst, srcp=srcp: e.tensor_copy(dst, srcp),
                          reads=("pb%d" % bank,), writes=(("aT%d_%d" % (g, b),) if tok else ()))

        def transposes_b16(src, src_tok, g, bank0, evac_flip):
            for b in range(4):
                bank = bank0 + (b % 2)
                pv = pb[bank][:, 0:256].bitcast(BF16)
                for kk in range(4):
                    o = pv[:, kk * 128:(kk + 1) * 128]
                    i_ = src[:, (4 * b + kk) * 128:(4 * b + kk + 1) * 128]
                    sc.op("pe", lambda e, o=o, i_=i_: e.transpose(o, i_, identb[:]),
                          reads=(src_tok, "identb"), writes=("pb%d" % bank,), signal=(kk == 3))
                dst = actT[:, 4 * b:4 * b + 4, g * 128:(g + 1) * 128]
                srcp = pv.rearrange("p (k t) -> p k t", k=4)
                if (b + evac_flip) % 2 == 0:
                    sc.op("act", lambda e, dst=dst, srcp=srcp: e.activation(out=dst, in_=srcp, func=AF.Copy),
                          reads=("pb%d" % bank,), writes=())
                else:
                    sc.op("dve", lambda e, dst=dst, srcp=srcp: e.tensor_copy(dst, srcp),
                          reads=("pb%d" % bank,), writes=())

        for sq in range(NSEQ):
            row0 = sq * S
            pa = PadAlloc()
            xs = [pa.f32(2048) for _ in range(4)]
            stg = [pa.bf16(4 * 512) for _ in range(2)]
            for g in range(NG):
                xb = xs[g % 4]
                src = x_d[row0 + g * 128: row0 + (g + 1) * 128, :]
                sc.dma("sp", lambda e, xb=xb, src=src: e.dma_start(out=xb, in_=src),
                       writes=("xs%d" % (g % 4),))
                transposes_f32(xb, "xs%d" % (g % 4), g, 2 * (g % 4), g, tok=True)
            if stop_phase <= 0:
                break
            nstage = [0]
            for p in range(16):
                s = wslot[0]
                wtoks = load_piece(awin, p * 512)
                if sq == 0 and p < 8:
                    wsrc = (awout if p < 4 else lwout)[:, (p % 4) * 512:(p % 4 + 1) * 512]
                    wdst = WB_d[p // 4][:, (p % 4) * 512:(p % 4 + 1) * 512]
                    sc.dma_bg("pool", lambda e, wdst=wdst, wsrc=wsrc: e.dma_start(out=wdst, in_=wsrc))
                kind_p = p // 4
                hp = p % 4
                for t in range(NT):
                    si = nstage[0] % 2
                    nstage[0] += 1
                    stv = stg[si].rearrange("p (j n) -> p j n", j=4)
                    stok = "stg%d" % si
                    for j in range(4):
                        bank = (4 * (t % 2) + j)
                        btok = "pb%d" % bank
                        for kc in range(16):
                            if kind_p < 2:
                                lhsT = wbuf[s][:, kc, j * 128:(j + 1) * 128]
                                rhs = actT[:, kc, t * 512:(t + 1) * 512]
                                atoks = tuple("aT%d_%d" % (4 * t + gg_, kc // 4) for gg_ in range(4))
                            else:
                                lhsT = actT[:, kc, t * 512 + j * 128: t * 512 + (j + 1) * 128]
                                rhs = wbuf[s][:, kc, :]
                                atoks = ("aT%d_%d" % (4 * t + j, kc // 4),)
                            sc.op("pe", lambda e, o=pb[bank][:, :], l=lhsT, r=rhs, kc=kc:
                                  e.matmul(o, l, r, start=(kc == 0), stop=(kc == 15)),
                                  reads=wtoks + (atoks if p == 0 else ()), writes=(btok,), signal=(kc == 15))
                        o = stv[:, j, :]
                        if kind_p == 0:
                            sc.op("act", lambda e, o=o, i_=pb[bank][:, :]: e.activation(
                                out=o, in_=i_, func=AF.Copy, scale=QSCALE),
                                reads=(btok,), writes=(stok,))
                        elif kind_p == 3:
                            sc.op("act", lambda e, o=o, i_=pb[bank][:, :]: e.activation(
                                out=o, in_=i_, func=AF.Silu), reads=(btok,), writes=(stok,))
                        else:
                            sc.op("dve", lambda e, o=o, i_=pb[bank][:, :]: e.tensor_copy(o, i_),
                                  reads=(btok,), writes=(stok,))
                    if kind_p < 2:
                        dd = (QT_d if kind_p == 0 else KT_d)[4 * hp:4 * hp + 4, :, t * 512:(t + 1) * 512]
                        dd = dd.rearrange("j p n -> p j n")
                    else:
                        dd = (V_d if kind_p == 2 else G_d)[t * 512:(t + 1) * 512, hp * 512:(hp + 1) * 512]
                        dd = dd.rearrange("(j p) n -> p j n", p=128)
                    sc.dma("sp", lambda e, dd=dd, stv=stv: e.dma_start(out=dd, in_=stv),
                           reads=(stok,), writes=())
                next_slot()
            sc.barrier()
            if stop_phase <= 1:
                continue

            pa = PadAlloc()
            ktb = [pa.bf16(S) for _ in range(2)]
            qtb = [pa.bf16(S) for _ in range(2)]
            vb = [pa.bf16(NG * 132).rearrange("p (t d) -> p t d", d=132) for _ in range(2)]
            gb = [pa.bf16(NG * 128).rearrange("p (t d) -> p t d", d=128) for _ in range(2)]
            tabb = [pa.bf16(640) for _ in range(2)]
            NSS = 5
            ssb = [pa.f32(512) for _ in range(NSS)]
            pTb = [pa.bf16(2560) for _ in range(2)]
            ogtok = [pa.bf16(128) for _ in range(8)]
            rcb = [pa.f32(2) for _ in range(8)]
            ogTs = [pa.bf16(S) for _ in range(2)]
            STB = (0, 1, 2, 5, 6)
            import os
            PREF = int(os.environ.get("PREF", "1"))
            NSTB = int(os.environ.get("NSTB", "5"))
            for i in range(2):
                sc.op("pool", lambda e, a=vb[i][:, :, 128:129]: e.memset(a, 1.0), writes=("v%d" % i,))

            def head_loads(h):
                hb = h % 2
                sc.dma("sp", lambda e, o=ktb[hb], i_=KT_d[h]: e.dma_start(out=o, in_=i_), writes=("kt%d" % hb,))
                sc.dma("sp", lambda e, o=qtb[hb], i_=QT_d[h]: e.dma_start(out=o, in_=i_), writes=("qt%d" % hb,))
                sc.dma("sp", lambda e, o=vb[hb][:, :, 0:128],
                       i_=V_d[:, h * 128:(h + 1) * 128].rearrange("(t p) d -> p t d", p=128):
                       e.dma_start(out=o, in_=i_), writes=("v%d" % hb,))
                sc.dma("sp", lambda e, o=gb[hb],
                       i_=G_d[:, h * 128:(h + 1) * 128].rearrange("(t p) d -> p t d", p=128):
                       e.dma_start(out=o, in_=i_), writes=("g%d" % hb,))
                sc.dma("pool", lambda e, o=tabb[hb], i_=tab_d[h]: e.dma_start(out=o, in_=i_), writes=("tab%d" % hb,))
                sc.op("pool", lambda e, a=tabb[hb][0:64, 576:640]: e.memset(a, NEG), writes=("tab%d" % hb,))
                sc.op("pool", lambda e, a=tabb[hb][64:128, 0:64]: e.memset(a, NEG), writes=("tab%d" % hb,))

            items2 = [(h, qi) for h in range(H) for qi in range(NT)]
            infos = {}
            nss = [0]

            def p2_f1(n):
                h, qi = items2[n]
                hb = h % 2
                if PREF and qi == 1 and h + 1 < H:
                    head_loads(h + 1)
                if (not PREF) and qi == 0 and h > 0:
                    head_loads(h)
                q0 = qi * 512
                pi = n % 2
                ptok = "pT%d" % pi
                info = {}
                off = 0
                for j in range(8):
                    if j in (2, 4, 6):
                        yield
                    J = 4 * qi - 4 + j
                    if J < 0:
                        continue
                    qlo = max(0, 2 * j - 8)
                    qhi = min(7, 2 * j + 1)
                    w = 64 * (qhi - qlo + 1)
                    c0 = 64 * (qlo - 2 * j + 8)
                    info[j] = (J, qlo, w, off)
                    k_ = nss[0]
                    nss[0] += 1
                    bank = STB[k_ % NSTB]
                    si = k_ % NSS
                    sc.op("pe", lambda e, o=pb[bank][:, 0:w], l=ktb[hb][:, J * 128:(J + 1) * 128],
                          r=qtb[hb][:, q0 + 64 * qlo: q0 + 64 * qlo + w]:
                          e.matmul(o, l, r, start=True, stop=False),
                          reads=("kt%d" % hb, "qt%d" % hb), writes=("pb%d" % bank,), signal=False)
                    sc.op("pe", lambda e, o=pb[bank][:, 0:w], l=identb[:], r=tabb[hb][:, c0:c0 + w]:
                          e.matmul(o, l, r, start=False, stop=True),
                          reads=("identb", "tab%d" % hb), writes=("pb%d" % bank,))
                    sc.op("act", lambda e, o=pTb[pi][:, off:off + w], a=pb[bank][:, 0:w]:
                          e.activation(out=o, in_=a, func=AF.Exp),
                          reads=("pb%d" % bank,), writes=(ptok,))
                    off += w
                infos[n] = info
                yield

            def p2_f2(n):
                h, qi = items2[n]
                hb = h % 2
                pi = n % 2
                ptok = "pT%d" % pi
                info = infos.pop(n)
                for g in range(4):
                    if g > 0:
                        yield
                    js = [j for j in range(g, g + 5) if j in info]
                    obank = 3 + (g % 2)
                    ocol = 0
                    otok = "pb%d" % obank
                    oap = pb[obank][:, ocol:ocol + 129]
                    for idx, j in enumerate(js):
                        J, qlo, w, poff = info[j]
                        c = poff + 64 * (2 * g - qlo)
                        sc.op("pe", lambda e, o=oap, l=pTb[pi][:, c:c + 128], r=vb[hb][:, J, 0:129],
                              st=(idx == 0), sp_=(idx == len(js) - 1):
                              e.matmul(o, l, r, start=st, stop=sp_),
                              reads=(ptok, "v%d" % hb), writes=(otok,), signal=(idx == len(js) - 1))
                    oi = (n % 2) * 4 + g
                    sc.op("dve", lambda e, o=rcb[oi][:, 0:1], a=pb[obank][:, ocol + 128:ocol + 129]:
                          e.reciprocal(o, a), reads=(otok,), writes=("rc%d" % oi,))
                    sc.op("dve", lambda e, o=ogtok[oi], a=pb[obank][:, ocol:ocol + 128], s_=rcb[oi][:, 0:1],
                          b=gb[hb][:, 4 * qi + g, :]:
                          e.scalar_tensor_tensor(o, a, s_, b, ALU.mult, ALU.mult),
                          reads=(otok, "rc%d" % oi, "g%d" % hb), writes=("ogtok%d" % oi,))

            def p2_f3(n):
                h, qi = items2[n]
                hb = h % 2
                q0 = qi * 512
                for g in range(4):
                    oi = (n % 2) * 4 + g
                    tpo = pb[7][:, 0:256].bitcast(BF16)[:, g * 128:(g + 1) * 128]
                    sc.op("pe", lambda e, o=tpo, a=ogtok[oi]: e.transpose(o, a, identb[:]),
                          reads=("ogtok%d" % oi, "identb"), writes=("pb7",), signal=(g == 3))
                sc.op("dve", lambda e, o=ogTs[hb][:, q0:q0 + 512], a=pb[7][:, 0:256].bitcast(BF16):
                      e.tensor_copy(o, a),
                      reads=("pb7",), writes=("ogTs%d" % hb,))
                if qi == NT - 1:
                    sc.dma("sp", lambda e, o=OGT_d[h * 128:(h + 1) * 128, :], a=ogTs[hb]:
                           e.dma_start(out=o, in_=a), reads=("ogTs%d" % hb,))

            head_loads(0)
            NI = len(items2)
            import os
            SK1 = int(os.environ.get("SK1", "1"))
            SK2 = int(os.environ.get("SK2", "2"))
            for step in range(NI + SK2):
                g1 = p2_f1(step) if step < NI else iter(())
                g2 = p2_f2(step - SK1) if 0 <= step - SK1 < NI else iter(())
                for _ in range(5):
                    next(g1, None)
                    next(g2, None)
                for _ in g1:
                    pass
                for _ in g2:
                    pass
                if 0 <= step - SK2 < NI:
                    p2_f3(step - SK2)
            sc.barrier(bg=True)
            if stop_phase <= 2:
                continue

            def outproj_ln(srcT_d, w_d, layer, resid_d, resid_row0, dst_d, dst_row0, do_T):
                pa = PadAlloc()
                ogt = pa.bf16(16 * 512).rearrange("p (k n) -> p k n", k=16)
                xz = [[pa.f32(2048) for _ in range(4)] for _ in range(2)]
                gbc = pa.f32(2048)
                bbc = pa.f32(2048)
                stats = [pa.f32(24) for _ in range(2)]
                mv = [pa.f32(8) for _ in range(2)]
                nbk = [0]
                sc.dma("sp", lambda e: e.dma_start(out=gbc, in_=lng_d[layer, :].partition_broadcast(128)),
                       writes=("gbc",))
                sc.dma("sp", lambda e: e.dma_start(out=bbc, in_=lnb_d[layer, :].partition_broadcast(128)),
                       writes=("bbc",))
                sv = srcT_d.rearrange("(kc p) n -> p kc n", p=128)
                xtok = lambda t, i: "xz%d_%d" % (t % 2, i)

                ogtb = None if do_T else [actT[:, :, 0:512], actT[:, :, 512:1024]]

                def ogt_of(t):
                    return ogt if do_T else ogtb[t % 2]

                def ogt_toks(t, i):
                    return ("ogtq%d" % i,) if do_T else ("ogtb%dh0" % (t % 2), "ogtb%dh1" % (t % 2))

                def ogt_load(t):
                    if do_T:
                        for i in range(4):
                            sc.dma("sp", lambda e, o=ogt[:, :, i * 128:(i + 1) * 128],
                                   i_=sv[:, :, t * 512 + i * 128: t * 512 + (i + 1) * 128]:
                                   e.dma_start(out=o, in_=i_), writes=("ogtq%d" % i,))
                    else:
                        for hf in range(2):
                            sc.dma("sp", lambda e, o=ogtb[t % 2][:, hf * 8:(hf + 1) * 8, :],
                                   i_=sv[:, hf * 8:(hf + 1) * 8, t * 512:(t + 1) * 512]:
                                   e.dma_start(out=o, in_=i_), writes=("ogtb%dh%d" % (t % 2, hf),))

                def xz_loads(t):
                    for i in range(4):
                        r0 = resid_row0 + t * 512 + i * 128
                        sc.dma("sp", lambda e, o=xz[t % 2][i], i_=resid_d[r0:r0 + 128, :]:
                               e.dma_start(out=o, in_=i_), writes=(xtok(t, i),))

                wq = []

                def prefetch(n):
                    s_ = wslot[0]
                    wq.append((s_, load_piece_bf16(w_d, n * 512)))
                    next_slot()

                def mm_piece(t, n):
                    s, wtoks = wq.pop(0)
                    for i in range(4):
                        bank = nbk[0] % 6
                        nbk[0] += 1
                        btok = "pb%d" % bank
                        for kc in range(16):
                            sc.op("pe", lambda e, o=pb[bank][:, :], l=ogt_of(t)[:, kc, i * 128:(i + 1) * 128],
                                  r=wbuf[s][:, kc, :], kc=kc:
                                  e.matmul(o, l, r, start=(kc == 0), stop=(kc == 15)),
                                  reads=wtoks + ogt_toks(t, i), writes=(btok,), signal=(kc == 15))
                        zz = xz[t % 2][i][:, n * 512:(n + 1) * 512]
                        sc.op("dve", lambda e, zz=zz, a=pb[bank][:, :]:
                              e.scalar_tensor_tensor(zz, zz, ALPHA, a, ALU.mult, ALU.add),
                              reads=(btok, xtok(t, i)), writes=(xtok(t, i),))
                    if 4 * t + n + 2 < 4 * NT:
                        prefetch((n + 2) % 4)

                def ln_group(t, i):
                    si = i % 2
                    xt = xtok(t, i)
                    z = xz[t % 2][i]
                    for c4 in range(4):
                        sc.op("dve", lambda e, o=stats[si][:, c4 * 6:(c4 + 1) * 6],
                              a=z[:, c4 * 512:(c4 + 1) * 512]: e.bn_stats(o, a),
                              reads=(xt,), writes=("stats%d" % si,))
                    sc.op("dve", lambda e, o=mv[si][:, 0:2], a=stats[si]: e.bn_aggr(o, a),
                          reads=("stats%d" % si,), writes=("mv%d" % si,))
                    sc.op("dve", lambda e, o=mv[si][:, 2:3], a=mv[si][:, 1:2]:
                          e.tensor_scalar(o, a, LN_EPS, None, ALU.add),
                          reads=("mv%d" % si,), writes=("mvb%d" % si,))
                    sc.op("act", lambda e, o=mv[si][:, 3:4], a=mv[si][:, 2:3]:
                          e.activation(out=o, in_=a, func=AF.Sqrt),
                          reads=("mvb%d" % si,), writes=("mvc%d" % si,))
                    sc.op("dve", lambda e, o=mv[si][:, 4:5], a=mv[si][:, 3:4]: e.reciprocal(o, a),
                          reads=("mvc%d" % si,), writes=("mvd%d" % si,))
                    sc.op("dve", lambda e, z=z, m=mv[si][:, 0:1]:
                          e.scalar_tensor_tensor(z, z, m, gbc, ALU.subtract, ALU.mult),
                          reads=(xt, "mv%d" % si, "gbc"), writes=(xt,))
                    sc.op("act", lambda e, z=z, r=mv[si][:, 4:5]:
                          e.activation(out=z, in_=z, func=AF.Identity, scale=r),
                          reads=(xt, "mvd%d" % si), writes=(xt,))
                    sc.op("pool", lambda e, z=z: e.tensor_tensor(z, z, bbc, ALU.add),
                          reads=(xt, "bbc"), writes=(xt,))
                    r0 = dst_row0 + t * 512 + i * 128
                    sc.dma("sp", lambda e, o=dst_d[r0:r0 + 128, :], a=z: e.dma_start(out=o, in_=a),
                           reads=(xt,))

                def tr_group(t, i):
                    transposes_f32(xz[t % 2][i], xtok(t, i), 4 * t + i, 6, 0, all_act=True)

                ogt_load(0)
                xz_loads(0)
                prefetch(0)
                prefetch(1)
                for t in range(NT):
                    mm_piece(t, 0)
                    if (not do_T) and t + 1 < NT:
                        ogt_load(t + 1)
                    if t > 0:
                        ln_group(t - 1, 0)
                        ln_group(t - 1, 1)
                    mm_piece(t, 1)
                    if t > 0:
                        ln_group(t - 1, 2)
                        ln_group(t - 1, 3)
                    mm_piece(t, 2)
                    if t > 0 and do_T:
                        for i in range(4):
                            tr_group(t - 1, i)
                    if t + 1 < NT:
                        xz_loads(t + 1)
                    mm_piece(t, 3)
                    if do_T and t + 1 < NT:
                        ogt_load(t + 1)
                for i in range(4):
                    ln_group(NT - 1, i)
                    if do_T:
                        tr_group(NT - 1, i)
                sc.barrier()

            outproj_ln(OGT_d, WB_d[0], 0, x_d, row0, H1_d, 0, True)
            if stop_phase <= 3:
                continue

            pa = PadAlloc()
            upad = [pa.f32(S + 4) for _ in range(2)]
            ysb = [pa.bf16(S) for _ in range(2)]
            NTB = 3
            uc = [pa.f32(512) for _ in range(NTB)]
            ucb = [pa.bf16(512) for _ in range(NTB)]
            rr = [pa.f32(512) for _ in range(NTB)]
            ii = [pa.f32(512) for _ in range(NTB)]
            aa = [pa.f32(512) for _ in range(NTB)]
            mm = [pa.f32(512) for _ in range(NTB)]
            bb = [pa.f32(512) for _ in range(NTB)]
            hh = [pa.f32(512) for _ in range(NTB)]
            sg = [pa.f32(512) for _ in range(NTB)]
            gg = [pa.f32(512) for _ in range(NTB)]
            for i in range(2):
                sc.op("pool", lambda e, a=upad[i][:, 0:3]: e.memset(a, 0.0), writes=("upad%d" % i,))
            wt4 = {}

            def p4_wload(c2):
                wt4[c2] = (wslot[0],
                           load_piece(lwin, c2 * 256, 256, 0) + load_piece(lwin, D + c2 * 256, 256, 256))
                next_slot()

            NI4 = 16 * NT

            def p4_ids(n):
                c, t = n // NT, n % NT
                return c, t, c % 2, n % NTB, "_%d" % (n % NTB)

            def p4_f1a(n):
                c, t, ub, tb, T = p4_ids(n)
                c2, jj = c // 2, c % 2
                if jj == 0 and t == 0 and c2 + 1 < 8:
                    p4_wload(c2 + 1)
                s, wtoks = wt4[c2]
                bu, bg = 2 * (n % 2), 2 * (n % 2) + 1
                for (bank, colb) in ((bu, jj * 128), (bg, 256 + jj * 128)):
                    for kc in range(16):
                        sc.op("pe", lambda e, o=pb[bank][:, :], l=wbuf[s][:, kc, colb:colb + 128],
                              r=actT[:, kc, t * 512:(t + 1) * 512], kc=kc:
                              e.matmul(o, l, r, start=(kc == 0), stop=(kc == 15)),
                              reads=wtoks, writes=("pb%d" % bank,), signal=(kc == 15))

            def p4_f1b(n):
                c, t, ub, tb, T = p4_ids(n)
                bu, bg = 2 * (n % 2), 2 * (n % 2) + 1
                utok = "upad%d" % ub
                sc.op("dve", lambda e, o=upad[ub][:, 3 + t * 512: 3 + (t + 1) * 512], a=pb[bu][:, :]:
                      e.tensor_copy(o, a),
                      reads=("pb%d" % bu,), writes=(utok + "_t%d" % t,))
                sc.op("act", lambda e, o=sg[tb], a=pb[bg][:, :]: e.activation(out=o, in_=a, func=AF.Tanh, scale=0.5),
                      reads=("pb%d" % bg,), writes=("sg" + T,))
                sc.op("dve", lambda e, o=gg[tb], a=pb[bg][:, :]: e.tensor_copy(o, a),
                      reads=("pb%d" % bg, "sg" + T), writes=("gg" + T,))

            def p4_f2(n):
                c, t, ub, tb, T = p4_ids(n)
                utok = "upad%d" % ub
                br, bi = 4 + 2 * (n % 2), 5 + 2 * (n % 2)
                ureads = (utok, utok + "_t%d" % t) + ((utok + "_t%d" % (t - 1),) if t > 0 else ())
                sc.op("dve", lambda e, o=uc[tb], a=upad[ub][:, t * 512: t * 512 + 512],
                      w0=vecs[:, 0, c:c + 1], cb=vecs[:, 4, c:c + 1]:
                      e.tensor_scalar(o, a, w0, cb, ALU.mult, ALU.add),
                      reads=ureads + ("vecs",), writes=("uc" + T,))
                for k in range(1, 4):
                    sc.op("dve", lambda e, o=uc[tb], a=upad[ub][:, t * 512 + k: t * 512 + k + 512],
                          wk=vecs[:, k, c:c + 1]:
                          e.scalar_tensor_tensor(o, a, wk, o, ALU.mult, ALU.add),
                          reads=ureads + ("vecs", "uc" + T), writes=("uc" + T,))
                sc.op("dve", lambda e, o=ucb[tb], a=uc[tb]: e.tensor_copy(o, a),
                      reads=("uc" + T,), writes=("ucb" + T,))
                sc.op("pe", lambda e, o=pb[br][:, :], l=wa_sb[:, c, :], r=ucb[tb]:
                      e.matmul(o, l, r, start=True, stop=True),
                      reads=("wa", "ucb" + T), writes=("pb%d" % br,))
                sc.op("pe", lambda e, o=pb[bi][:, :], l=wx_sb[:, c, :], r=ucb[tb]:
                      e.matmul(o, l, r, start=True, stop=True),
                      reads=("wx", "ucb" + T), writes=("pb%d" % bi,))

            def p4_f3a(n):
                c, t, ub, tb, T = p4_ids(n)
                br, bi = 4 + 2 * (n % 2), 5 + 2 * (n % 2)
                sc.op("act", lambda e, o=rr[tb], a=pb[br][:, :], b=hb[:, 0, c:c + 1]:
                      e.activation(out=o, in_=a, func=AF.Tanh, bias=b, scale=0.5),
                      reads=("pb%d" % br, "hb"), writes=("rr" + T,))
                sc.op("act", lambda e, o=ii[tb], a=pb[bi][:, :], b=hb[:, 1, c:c + 1]:
                      e.activation(out=o, in_=a, func=AF.Tanh, bias=b, scale=0.5),
                      reads=("pb%d" % bi, "hb"), writes=("ii" + T,))
                sc.op("act", lambda e, o=aa[tb], a=rr[tb], s_=cl[:, 1, c:c + 1]:
                      e.activation(out=o, in_=a, func=AF.Exp, scale=s_, bias=s_),
                      reads=("rr" + T, "cl1"), writes=("aa" + T,))
                sc.op("act", lambda e, o=mm[tb], a=rr[tb], s_=cl[:, 0, c:c + 1]:
                      e.activation(out=o, in_=a, func=AF.Exp, scale=s_, bias=s_),
                      reads=("rr" + T, "cl0"), writes=("mm" + T,))
                sc.op("dve", lambda e, o=bb[tb], a=ii[tb], b=uc[tb]:
                      e.scalar_tensor_tensor(o, a, 1.0, b, ALU.add, ALU.mult),
                      reads=("ii" + T, "uc" + T), writes=("bb" + T,))
                sc.op("act", lambda e, o=mm[tb]: e.activation(out=o, in_=o, func=AF.Abs, scale=-0.25, bias=0.25),
                      reads=("mm" + T,), writes=("mm" + T,))
                sc.op("act", lambda e, o=mm[tb]: e.activation(out=o, in_=o, func=AF.Sqrt),
                      reads=("mm" + T,), writes=("mm" + T,))
                sc.op("pool", lambda e, o=bb[tb], a=mm[tb]: e.tensor_tensor(o, o, a, ALU.mult),
                      reads=("bb" + T, "mm" + T), writes=("bb" + T,))
                sc.op("pool", lambda e, o=sg[tb]: e.tensor_scalar(o, o, 0.5, 0.5, ALU.mult, ALU.add),
                      reads=("sg" + T,), writes=("sg" + T,))
                sc.op("pool", lambda e, o=sg[tb], a=gg[tb]: e.tensor_tensor(o, o, a, ALU.mult),
                      reads=("sg" + T, "gg" + T), writes=("sg" + T,))

            def p4_f3b(n):
                c, t, ub, tb, T = p4_ids(n)
                ptb = (n - 1) % NTB
                init = 0.0 if t == 0 else hh[ptb][:, 511:512]
                sc.op("dve", lambda e, o=hh[tb], a=aa[tb], b=bb[tb], init=init:
                      e.tensor_tensor_scan(o, a, b, init, ALU.mult, ALU.add),
                      reads=("aa" + T, "bb" + T) + (("hh_%d" % ptb,) if t > 0 else ()),
                      writes=("hh" + T,))
                sc.op("pool", lambda e, o=ysb[ub][:, t * 512:(t + 1) * 512], a=hh[tb], b=sg[tb]:
                      e.tensor_tensor(o, a, b, ALU.mult),
                      reads=("hh" + T, "sg" + T), writes=("ysb%d" % ub,))
                if t == NT - 1:
                    sc.dma("sp", lambda e, o=YT_d[c * 128:(c + 1) * 128, :], a=ysb[ub]: e.dma_start(out=o, in_=a),
                           reads=("ysb%d" % ub,))

            p4_wload(0)
            for step in range(NI4 + 2):
                if step < NI4:
                    p4_f1a(step)
                if 0 <= step - 2 < NI4:
                    p4_f3a(step - 2)
                if 0 <= step - 1 < NI4:
                    p4_f2(step - 1)
                if 0 <= step - 2 < NI4:
                    p4_f3b(step - 2)
                if step < NI4:
                    p4_f1b(step)
            sc.barrier()
            if stop_phase <= 4:
                continue
            outproj_ln(YT_d, WB_d[1], 1, H1_d, 0, out_d, row0, False)

        sc.barrier()
        sc.emit_all()
    return nc


def _prep_shared(inp):
    f = lambda a: np.ascontiguousarray(np.asarray(a, dtype=np.float32))
    k = np.arange(128)[:, None]
    xq = np.arange(640)[None, :]
    idx = np.minimum(xq - k, 256) + 256
    tab = f(np.asarray(inp["attn_rel_bias"])[0][:, idx])
    cw = np.asarray(inp["lru_conv_w"])[0]
    vec = np.stack([cw[0], cw[1], cw[2], cw[3], np.asarray(inp["lru_conv_b"])[0],
                    np.asarray(inp["lru_ba"])[0].reshape(-1), np.asarray(inp["lru_bx"])[0].reshape(-1),
                    np.asarray(inp["lru_lambda"])[0]], axis=0)
    vecs = f(vec.reshape(8, 16, 128).transpose(2, 0, 1))
    return {
        "awin": f(np.asarray(inp["attn_w_in"])[0]), "awout": f(np.asarray(inp["attn_w_out"])[0]),
        "tab": tab, "lwin": f(np.asarray(inp["lru_w_in"])[0]), "lwout": f(np.asarray(inp["lru_w_out"])[0]),
        "wa": f(np.asarray(inp["lru_wa"])[0]), "wx": f(np.asarray(inp["lru_wx"])[0]), "vecs": vecs,
        "lng": f(inp["ln_gain"]), "lnb": f(inp["ln_bias"]), "ident": np.eye(128, dtype=np.float32),
    }


def kernel(**inputs):
    x = np.asarray(inputs["x"], dtype=np.float32)
    B, S, _ = x.shape
    nseq = B // N_CORES
    shared = _prep_shared(inputs)
    nc = build(NSEQ=nseq, S=S)
    in_maps = []
    for c in range(N_CORES):
        m = dict(shared)
        m["x"] = np.ascontiguousarray(x[c * nseq:(c + 1) * nseq].reshape(nseq * S, D))
        in_maps.append(m)
    res = run_bass_kernel_spmd(nc, in_maps, core_ids=list(range(N_CORES)))
    out = np.concatenate([np.asarray(r["out"]).reshape(nseq, S, D) for r in res.results], axis=0)
    return out.astype(np.float32)
```
